# Optimizing a Trainium2 kernel written in Bass

```python
import jax, jax.numpy as jnp
from jax import lax
import numpy as np

D_MODEL = 1024
BATCH = 8
SEQ = 4096
DEPTH = 1

MEM_LEN = 256
MIX_WIDTH = D_MODEL
CONV_CH = MIX_WIDTH // 2
CONV_WIDTH = 31
CONV_PAD = CONV_WIDTH // 2
HEAD_DIM = 64
ATTN_CH = MIX_WIDTH - CONV_CH
N_Q_HEADS = ATTN_CH // HEAD_DIM
N_KV_HEADS = max(N_Q_HEADS // 4, 1)
KV_CH = N_KV_HEADS * HEAD_DIM
WINDOW = 128
BLOCK = 128
ROPE_THETA = 10000.0
MEM_HEADS = 4
MEM_HEAD_DIM = D_MODEL // MEM_HEADS
D_FF = -(-8 * D_MODEL // (3 * 256)) * 256
IN_COLS = 2 * CONV_CH + ATTN_CH + 2 * KV_CH
EPS = 1e-6

kernel_name = "hybrid_conformer_swa_encoder_block"


def rms_norm(x, g):
    xf = x.astype(jnp.float32)
    y = xf * lax.rsqrt(jnp.mean(xf * xf, axis=-1, keepdims=True) + EPS)
    return (y * g.astype(jnp.float32)).astype(x.dtype)


def layer_norm(x, g, b):
    xf = x.astype(jnp.float32)
    mu = jnp.mean(xf, axis=-1, keepdims=True)
    var = jnp.mean(jnp.square(xf - mu), axis=-1, keepdims=True)
    y = (xf - mu) * lax.rsqrt(var + EPS)
    return (y * g.astype(jnp.float32) + b.astype(jnp.float32)).astype(x.dtype)


def rope_tables(seq):
    pos = jnp.arange(seq, dtype=jnp.float32)
    inv_freq = ROPE_THETA ** (-jnp.arange(0, HEAD_DIM, 2, dtype=jnp.float32) / HEAD_DIM)
    ang = pos[:, None] * inv_freq[None, :]
    return jnp.cos(ang), jnp.sin(ang)


def apply_rope(t, cos, sin):
    t1, t2 = jnp.split(t.astype(jnp.float32), 2, axis=-1)
    c = cos[None, :, None, :]
    s = sin[None, :, None, :]
    return jnp.concatenate([t1 * c - t2 * s, t1 * s + t2 * c], axis=-1).astype(t.dtype)


def conformer_conv_group(u_glu, w_dw, b_dw, ln_g, ln_b):
    a, gate = jnp.split(u_glu, 2, axis=-1)
    v = a * jax.nn.sigmoid(gate)
    y = lax.conv_general_dilated(
        v, w_dw[:, None, :].astype(v.dtype), window_strides=(1,),
        padding=[(CONV_PAD, CONV_PAD)], dimension_numbers=("NWC", "WIO", "NWC"),
        feature_group_count=CONV_CH) + b_dw
    return jax.nn.silu(layer_norm(y, ln_g, ln_b))


def windowed_gqa_with_sink(q, k, v, sink):
    B, S = q.shape[0], q.shape[1]
    nb = S // BLOCK
    G = N_Q_HEADS // N_KV_HEADS
    qb = q.reshape(B, nb, BLOCK, N_KV_HEADS, G, HEAD_DIM)
    pad = ((0, 0), (BLOCK, BLOCK), (0, 0), (0, 0))
    kp = jnp.pad(k, pad).reshape(B, nb + 2, BLOCK, N_KV_HEADS, HEAD_DIM)
    vp = jnp.pad(v, pad).reshape(B, nb + 2, BLOCK, N_KV_HEADS, HEAD_DIM)
    kb = jnp.concatenate([kp[:, :-2], kp[:, 1:-1], kp[:, 2:]], axis=2)
    vb = jnp.concatenate([vp[:, :-2], vp[:, 1:-1], vp[:, 2:]], axis=2)
    a_idx = jnp.arange(BLOCK)[:, None]
    c_idx = jnp.arange(3 * BLOCK)[None, :]
    rel = c_idx - BLOCK - a_idx
    kpos = jnp.arange(nb)[:, None] * BLOCK - BLOCK + jnp.arange(3 * BLOCK)[None, :]
    in_seq = (kpos >= 0) & (kpos < S)
    mask = (jnp.abs(rel) <= WINDOW)[None] & in_seq[:, None, :]
    scale = HEAD_DIM ** -0.5
    s = jnp.einsum("bnqhgd,bnkhd->bnhgqk", qb, kb,
                   preferred_element_type=jnp.float32) * scale
    s = jnp.where(mask[None, :, None, None], s, -jnp.inf)
    sk = sink.astype(jnp.float32).reshape(N_KV_HEADS, G)[None, None, :, :, None, None]
    m = jnp.maximum(jnp.max(s, axis=-1, keepdims=True), sk)
    p = jnp.exp(s - m)
    denom = jnp.sum(p, axis=-1, keepdims=True) + jnp.exp(sk - m)
    o = jnp.einsum("bnhgqk,bnkhd->bnqhgd", (p / denom).astype(v.dtype), vb)
    return o.reshape(B, S, N_Q_HEADS * HEAD_DIM)


def memory_cross_attention(h, mem_n, w_q, w_kv, w_o):
    B, S = h.shape[0], h.shape[1]
    M = mem_n.shape[1]
    q = (h @ w_q).reshape(B, S, MEM_HEADS, MEM_HEAD_DIM)
    km, vm = jnp.split(mem_n @ w_kv, 2, axis=-1)
    km = km.reshape(B, M, MEM_HEADS, MEM_HEAD_DIM)
    vm = vm.reshape(B, M, MEM_HEADS, MEM_HEAD_DIM)
    s = jnp.einsum("bshd,bmhd->bhsm", q, km,
                   preferred_element_type=jnp.float32) * (MEM_HEAD_DIM ** -0.5)
    p = jax.nn.softmax(s, axis=-1)
    o = jnp.einsum("bhsm,bmhd->bshd", p.astype(vm.dtype), vm).reshape(B, S, D_MODEL)
    return o @ w_o


def swiglu(h, w_gate, w_up, w_down):
    return (jax.nn.silu(h @ w_gate) * (h @ w_up)) @ w_down


def setup_inputs(seed: int = 0) -> dict:
    key = jax.random.key(seed)
    ks = jax.random.split(key, 24)
    L = DEPTH

    def nrm(k, shape, scale):
        return jax.random.normal(k, shape, jnp.float32) * scale

    def gain(k, shape):
        return 1.0 + 0.05 * jax.random.normal(k, shape, jnp.float32)

    return {
        "x": nrm(ks[0], (BATCH, SEQ, D_MODEL), 1.0),
        "mem": nrm(ks[1], (BATCH, MEM_LEN, D_MODEL), 1.0),
        "g_mix": gain(ks[2], (L, D_MODEL)),
        "w_in": nrm(ks[3], (L, D_MODEL, IN_COLS), D_MODEL ** -0.5),
        "b_in": nrm(ks[4], (L, IN_COLS), 0.02),
        "w_dw": nrm(ks[5], (L, CONV_WIDTH, CONV_CH), CONV_WIDTH ** -0.5),
        "b_dw": nrm(ks[6], (L, CONV_CH), 0.02),
        "g_conv_ln": gain(ks[7], (L, CONV_CH)),
        "b_conv_ln": nrm(ks[8], (L, CONV_CH), 0.02),
        "attn_sink": nrm(ks[9], (L, N_Q_HEADS), 0.5),
        "w_out": nrm(ks[10], (L, MIX_WIDTH, D_MODEL), MIX_WIDTH ** -0.5),
        "b_out": nrm(ks[11], (L, D_MODEL), 0.02),
        "g_mem_q": gain(ks[12], (L, D_MODEL)),
        "g_mem_kv": gain(ks[13], (L, D_MODEL)),
        "w_mem_q": nrm(ks[14], (L, D_MODEL, D_MODEL), D_MODEL ** -0.5),
        "w_mem_kv": nrm(ks[15], (L, D_MODEL, 2 * D_MODEL), D_MODEL ** -0.5),
        "w_mem_o": nrm(ks[16], (L, D_MODEL, D_MODEL), D_MODEL ** -0.5),
        "g_ffn": gain(ks[17], (L, D_MODEL)),
        "w_gate": nrm(ks[18], (L, D_MODEL, D_FF), D_MODEL ** -0.5),
        "w_up": nrm(ks[19], (L, D_MODEL, D_FF), D_MODEL ** -0.5),
        "w_down": nrm(ks[20], (L, D_FF, D_MODEL), D_FF ** -0.5),
        "g_final": gain(ks[21], (D_MODEL,)),
    }


def reference(x, mem, g_mix, w_in, b_in, w_dw, b_dw, g_conv_ln, b_conv_ln, attn_sink,
              w_out, b_out, g_mem_q, g_mem_kv, w_mem_q, w_mem_kv, w_mem_o,
              g_ffn, w_gate, w_up, w_down, g_final):
    B, S = x.shape[0], x.shape[1]
    cos, sin = rope_tables(S)
    split_pts = [2 * CONV_CH, 2 * CONV_CH + ATTN_CH, 2 * CONV_CH + ATTN_CH + KV_CH]
    for l in range(DEPTH):
        h = rms_norm(x, g_mix[l])
        u = h @ w_in[l] + b_in[l]
        u_glu, q, k, v = jnp.split(u, split_pts, axis=-1)
        y_conv = conformer_conv_group(u_glu, w_dw[l], b_dw[l], g_conv_ln[l], b_conv_ln[l])
        q = apply_rope(q.reshape(B, S, N_Q_HEADS, HEAD_DIM), cos, sin)
        k = apply_rope(k.reshape(B, S, N_KV_HEADS, HEAD_DIM), cos, sin)
        v = v.reshape(B, S, N_KV_HEADS, HEAD_DIM)
        y_attn = windowed_gqa_with_sink(q, k, v, attn_sink[l])
        y_mix = jnp.concatenate([y_conv, y_attn], axis=-1)
        x = x + (y_mix @ w_out[l] + b_out[l])
        x = x + memory_cross_attention(rms_norm(x, g_mem_q[l]), rms_norm(mem, g_mem_kv[l]),
                                       w_mem_q[l], w_mem_kv[l], w_mem_o[l])
        x = x + swiglu(rms_norm(x, g_ffn[l]), w_gate[l], w_up[l], w_down[l])
    return rms_norm(x, g_final)
```

```python
import numpy as np
import concourse.bass as bass
import concourse.mybir as mybir
from concourse.bass_utils import run_bass_kernel_spmd

F32 = mybir.dt.float32
BF16 = mybir.dt.bfloat16
AF = mybir.ActivationFunctionType
ALU = mybir.AluOpType

P = 128
S = 4096
D = 1024
KC = 8
T = 512
NG = S // T
NB = S // P
DFF = 2816
NFC = DFF // P
EPS = 1e-6
MEM = 256


class Atom:
    __slots__ = ("w", "r")

    def __init__(self):
        self.w = None
        self.r = []


class Buf:
    def __init__(self, atoms=None, excl=False):
        self.atoms = atoms if atoms is not None else [Atom()]
        self.excl = excl


def bufs(n):
    return [Buf() for _ in range(n)]


class DmaSem:
    def __init__(self, nc, name):
        self.sem = nc.alloc_semaphore(name)
        self.count = 0


class Op:
    __slots__ = ("e", "idx", "fn", "deps", "dma", "me", "waits", "signal")


class Sched:
    ENG = ["pe", "act", "dve", "pool", "sp"]
    NEAR = {"pe": 0, "act": 1 << 30, "dve": 1 << 30, "pool": 1 << 30, "sp": 0}

    def __init__(self, nc):
        self.nc = nc
        self.eng = {"pe": nc.tensor, "act": nc.scalar, "dve": nc.vector, "pool": nc.gpsimd, "sp": nc.sync}
        self.ops = []
        self.count = {e: 0 for e in self.ENG}
        self.sems = {e: nc.alloc_semaphore("prog_" + e) for e in self.ENG}

    def op(self, e, fn, reads=(), writes=(), dma=None):
        xr = [b for b in reads if b.excl and b not in writes]
        if xr:
            writes = list(writes) + xr
            reads = [b for b in reads if not b.excl]
        deps = set()
        for b in reads:
            for a in b.atoms:
                if a.w is not None:
                    deps.add(a.w)
        for b in writes:
            for a in b.atoms:
                if a.w is not None:
                    deps.add(a.w)
                deps.update(a.r)
        o = Op()
        o.e = e
        o.idx = self.count[e]
        self.count[e] += 1
        o.fn = fn
        o.dma = dma
        if dma is None:
            o.me = ("e", e, o.idx)
        else:
            dma.count += 16
            o.me = ("d", dma, dma.count)
        o.deps = deps
        o.waits = []
        o.signal = False
        self.ops.append(o)
        for b in reads:
            for a in b.atoms:
                a.r.append(o.me)
        for b in writes:
            for a in b.atoms:
                a.w = o.me
                a.r = []
        return o

    def finalize(self):
        known = {e: {} for e in self.ENG}
        targets = set()
        for o in self.ops:
            for d in o.deps:
                if d[0] == "e":
                    targets.add((d[1], d[2]))
        snap = {}
        opmap = {}
        for o in self.ops:
            if o.dma is None:
                opmap[(o.e, o.idx)] = o
        for o in self.ops:
            kn = known[o.e]
            need = {}
            for d in o.deps:
                if d[0] == "e":
                    _, x, j = d
                    if x == o.e:
                        if o.idx - j > self.NEAR[x]:
                            continue
                    if kn.get(x, -1) >= j:
                        continue
                    if need.get(x, -1) < j:
                        need[x] = j
                else:
                    _, ds, v = d
                    if kn.get(ds, -1) >= v:
                        continue
                    if need.get(ds, -1) < v:
                        need[ds] = v
            for k, v in need.items():
                o.waits.append((k, v))
                if kn.get(k, -1) < v:
                    kn[k] = v
                if isinstance(k, str):
                    opmap[(k, v)].signal = True
                    sn = snap.get((k, v))
                    if sn is not None:
                        for kk, vv in sn.items():
                            if kn.get(kk, -1) < vv:
                                kn[kk] = vv
            if o.dma is None:
                if (o.e, o.idx) in targets:
                    sn = dict(kn)
                    sn[o.e] = max(sn.get(o.e, -1), o.idx - 1)
                    snap[(o.e, o.idx)] = sn
        rank = {}
        cnt = {e: 0 for e in self.ENG}
        for o in self.ops:
            if o.dma is None and o.signal:
                cnt[o.e] += 1
                rank[(o.e, o.idx)] = cnt[o.e]
        nw = 0
        for o in self.ops:
            eng = self.eng[o.e]
            for k, v in o.waits:
                if isinstance(k, str):
                    eng.wait_ge(self.sems[k], rank[(k, v)])
                else:
                    eng.wait_ge(k.sem, v)
                nw += 1
            ins = o.fn()
            if o.dma is not None:
                ins.then_inc(o.dma.sem, 16)
            elif o.signal:
                ins.then_inc(self.sems[o.e], 1)
        self.stats = dict(n_ops=len(self.ops), n_waits=nw, signals=cnt)

    def final_wait(self, e, dsems):
        eng = self.eng[e]
        for ds in dsems:
            if ds.count > 0:
                eng.wait_ge(ds.sem, ds.count)


def build_program(debug_names=()):
    nc = bass.Bass("TRN2", target_bir_lowering=False)
    sc = Sched(nc)
    dbg = {}

    def dram_in(name, shape, dt=F32):
        return nc.dram_tensor(name, list(shape), dt, kind="ExternalInput").ap()

    x_d = dram_in("x", [S, D])
    mem_d = dram_in("mem", [MEM, D])
    g_mix_d = dram_in("g_mix", [D])
    w_in_d = dram_in("w_in", [D, 1792])
    b_in_d = dram_in("b_in", [1792])
    w_dw_d = dram_in("w_dw", [31, 512])
    b_dw_d = dram_in("b_dw", [512])
    g_ln_d = dram_in("g_conv_ln", [512])
    b_ln_d = dram_in("b_conv_ln", [512])
    sink_d = dram_in("attn_sink", [8])
    w_out_d = dram_in("w_out", [D, D])
    b_out_d = dram_in("b_out", [D])
    g_mq_d = dram_in("g_mem_q", [D])
    g_mkv_d = dram_in("g_mem_kv", [D])
    w_mq_d = dram_in("w_mem_q", [D, D])
    w_mkv_d = dram_in("w_mem_kv", [D, 2 * D])
    w_mo_d = dram_in("w_mem_o", [D, D])
    g_ffn_d = dram_in("g_ffn", [D])
    w_gate_d = dram_in("w_gate", [D, DFF])
    w_up_d = dram_in("w_up", [D, DFF])
    w_down_d = dram_in("w_down", [DFF, D])
    g_fin_d = dram_in("g_final", [D])
    ropec_d = dram_in("rope_cc", [S, 64])
    ropes_d = dram_in("rope_ss", [S, 64])
    ident_d = dram_in("c_ident", [P, P])
    mask_d = dram_in("c_mask", [P, 256])
    out_d = nc.dram_tensor("out", [S, D], F32, kind="ExternalOutput").ap()

    UNIT = 4096
    unit_names = (["q", "kv", "glu0", "glu1", "kvm0", "kvm1", "kvm2", "kvm3", "out0", "out1",
                   "wq0", "wq1", "wo0", "wo1"] + ["gu%d" % m for m in range(11)] + ["wd%d" % m for m in range(6)] + ["dg%d" % c for c in range(4)])
    scr = nc.dram_tensor("wscratch", [len(unit_names), P, UNIT], BF16, kind="Internal").ap()
    uidx = {n: i for i, n in enumerate(unit_names)}
    scr_buf = {n: Buf() for n in unit_names}
    scr_sem = {n: DmaSem(nc, "cv_" + n) for n in unit_names}

    def kcview(w, c0, c1):
        return w.rearrange("(kc p) n -> p kc n", p=P)[:, :, c0:c1]

    conv_hist = []

    def conv(name, dst_view, src_view):
        rd = [conv_hist[-6]] if len(conv_hist) >= 6 else []
        sc.op("pool", lambda: nc.gpsimd.dma_start(out=dst_view, in_=src_view),
              reads=rd, writes=[], dma=scr_sem[name])

    def conv_done(name):
        me = ("d", scr_sem[name], scr_sem[name].count)
        for a in scr_buf[name].atoms:
            a.w = me
        conv_hist.append(scr_buf[name])

    def unit3(name, ncols):
        return scr[uidx[name]][:, 0:KC * ncols].rearrange("p (kc n) -> p kc n", kc=KC)

    conv_order = ["kvm0", "kvm1", "kvm2", "kvm3", "out0", "out1",
                  "wq0", "wq1", "wo0", "wo1"] + ["gu%d" % m for m in range(11)] + ["wd%d" % m for m in range(6)]
    conv_emitted = [0]

    def emit_one_conv(n):
        if n in ("q", "kv"):
            c0, c1, w = (1024, 1536, 512) if n == "q" else (1536, 1792, 256)
            conv(n, unit3(n, w), kcview(w_in_d, c0, c1))
        elif n.startswith("glu"):
            u = int(n[3:])
            conv(n, unit3(n, 512)[:, :, 0:256], kcview(w_in_d, 256 * u, 256 * u + 256))
            conv(n, unit3(n, 512)[:, :, 256:512], kcview(w_in_d, 512 + 256 * u, 512 + 256 * u + 256))
        elif n.startswith("kvm"):
            u = int(n[3:])
            conv(n, unit3(n, 512), kcview(w_mkv_d, 512 * u, 512 * u + 512))
        elif n.startswith("out"):
            u = int(n[3:])
            conv(n, unit3(n, 512), kcview(w_out_d, 512 * u, 512 * u + 512))
        elif n.startswith("wq"):
            u = int(n[2:])
            conv(n, unit3(n, 512), kcview(w_mq_d, 512 * u, 512 * u + 512))
        elif n.startswith("wo"):
            u = int(n[2:])
            conv(n, unit3(n, 512), kcview(w_mo_d, 512 * u, 512 * u + 512))
        elif n.startswith("gu"):
            m = int(n[2:])
            v = scr[uidx[n]].rearrange("p (g kc n) -> p g kc n", g=2, kc=KC)
            conv(n, v[:, 0], kcview(w_gate_d, 256 * m, 256 * m + 256))
            conv(n, v[:, 1], kcview(w_up_d, 256 * m, 256 * m + 256))
        elif n.startswith("wd"):
            m = int(n[2:])
            nch = 4 if m < 5 else 2
            v = scr[uidx[n]][:, 0:nch * D].rearrange("p (c n) -> p c n", c=nch)
            src = w_down_d[m * 4 * P:(m * 4 + nch) * P, :].rearrange("(c p) n -> p c n", p=P)
            conv(n, v, src)
        conv_done(n)

    def ensure_conv(upto):
        while conv_emitted[0] <= min(upto, len(conv_order) - 1):
            emit_one_conv(conv_order[conv_emitted[0]])
            conv_emitted[0] += 1

    def sb(name, shape, dt):
        return nc.alloc_sbuf_tensor(name, list(shape), dt)

    ident = sb("ident", [P, P], BF16); ident_b = Buf()
    masks = sb("masks", [P, 2, P], BF16); masks_b = Buf()
    ones_pad = sb("ones_pad", [P, P], BF16); ones_pad_b = Buf()
    ones_full = sb("ones_full", [P, P], BF16); ones_full_b = Buf()
    bqkv_pad = sb("bqkv_pad", [P, 768], BF16); bqkv_b = Buf()
    bout_pad = sb("bout_pad", [P, D], BF16); bout_b = Buf()
    mhalf = sb("mhalf", [P, 8], F32); mhalf_b = Buf()
    gfin = sb("gfin", [P, D], F32)
    epsc = sb("epsc", [P, 8], F32)
    vec1 = sb("vec1", [P, 4, 40], F32)
    vec2 = sb("vec2", [P, KC, 4], F32)
    sinkt = sb("sinkt", [P, 8], F32)
    expsink = sb("expsink", [P, 8], F32); expsink_b = Buf()
    consts_b = Buf()
    cst0_b = Buf()
    identf = sb("identf", [P, P], F32)
    vecd_b = Buf()
    ropec = [sb("ropec%d" % i, [P, 4, 64], F32) for i in range(2)]
    ropes = [sb("ropes%d" % i, [P, 4, 64], F32) for i in range(2)]
    rope_b = bufs(2)
    rope_sem = [DmaSem(nc, "rope%d" % i) for i in range(2)]

    KmT = sb("KmT", [P, KC, MEM], BF16); KmT_b = bufs(KC)
    Vm = sb("Vm", [P, 2, D], BF16); Vm_b = [bufs(2) for _ in range(2)]

    GLW = 16 + T + 16
    gl = [sb("gl%d" % i, [P, 4, GLW], BF16) for i in range(2)]
    gl_c = [bufs(4) for _ in range(2)]
    gl_lh = bufs(2)
    gl_rh = bufs(2)
    qT = [sb("qT%d" % i, [P, 4, T], BF16) for i in range(2)]
    qT_b = [bufs(4) for _ in range(2)]
    kTz = [sb("kTz%d" % i, [P, 4, T], BF16) for i in range(3)]
    kTz_b = [bufs(4) for _ in range(3)]
    Vr = [sb("Vr%d" % i, [P, 4, 2, 65], BF16) for i in range(3)]
    Vr_b = [bufs(4) for _ in range(3)]

    xres = sb("xres", [P, 4, D], F32); xres_b = bufs(4)
    xres_sem = [DmaSem(nc, "xres%d" % i) for i in range(4)]
    out_sem = [DmaSem(nc, "outs%d" % i) for i in range(4)]
    xld = [sb("xld%d" % i, [P, D], F32) for i in range(2)]; xld_b = bufs(2)
    xld_sem = [DmaSem(nc, "xld%d" % i) for i in range(2)]
    xnb = [sb("xnb%d" % i, [P, D], BF16) for i in range(2)]; xnb_b = bufs(2)
    stat = [sb("stat%d" % i, [P, 4], F32) for i in range(2)]
    stat_b = [bufs(3) for _ in range(2)]
    hT = [sb("hT%d" % i, [P, KC, T], BF16) for i in range(3)]
    hT_b = [bufs(4) for _ in range(3)]

    NSLOT = 4
    wslot = [sb("wslot%d" % i, [P, UNIT], BF16) for i in range(NSLOT)]
    wslot_b = bufs(NSLOT)
    wslot_sem = [DmaSem(nc, "ws%d" % i) for i in range(NSLOT)]

    tgA = [sb("tgA%d" % i, [P, T], F32) for i in range(2)]; tgA_b = bufs(2)
    ahA = [sb("ahA%d" % i, [P, T], F32) for i in range(2)]; ahA_b = bufs(2)
    rtA = sb("rtA", [P, 640], F32); rtA_b = Buf()
    rtB = sb("rtB", [P, 640], F32); rtB_b = Buf()
    qrot = [sb("qrot%d" % i, [P, T], BF16) for i in range(2)]; qrot_b = bufs(2)
    kpad = [sb("kpad%d" % i, [P, 4, P], BF16) for i in range(2)]; kpad_b = bufs(2)

    ARENA_KB = 46
    arena = sb("arena", [P, ARENA_KB * 512], BF16)
    atoms = [Atom() for _ in range(ARENA_KB)]

    def carve(kb0, nkb, dt, shape_free):
        ap = arena[:, kb0 * 512:(kb0 + nkb) * 512]
        if dt == F32:
            ap = ap.bitcast(F32)
        if len(shape_free) == 2:
            ap = ap.rearrange("p (a b) -> p a b", a=shape_free[0])
        return ap

    def abuf(kb0, nkb):
        return Buf(atoms[kb0:kb0 + nkb])

    acc = carve(30, 8, F32, [4, T]); acc_b = [abuf(30 + 2 * c, 2) for c in range(4)]
    ybf = carve(38, 4, BF16, [4, T]); ybf_b = [abuf(38 + c, 1) for c in range(4)]
    ysq = carve(42, 4, BF16, [4, T]); ysq_b = [abuf(42 + c, 1) for c in range(4)]
    mean = carve(8, 2, F32, [T]); mean_b = abuf(8, 2)
    var = carve(10, 2, F32, [T]); var_b = abuf(10, 2)
    rstd = carve(12, 2, F32, [T]); rstd_b = abuf(12, 2)
    dtm = carve(14, 2, F32, [T]); dtm_b = abuf(14, 2)
    thm = carve(16, 2, F32, [T]); thm_b = abuf(16, 2)
    zhm = carve(18, 2, F32, [T]); zhm_b = abuf(18, 2)
    PT = [carve(20 + 3 * i, 3, BF16, [1536]) for i in range(2)]; PT_b = [abuf(20 + 3 * i, 3) for i in range(2)]
    yatt = carve(26, 4, BF16, [4, T]); yatt_b = [abuf(26 + t, 1) for t in range(4)]
    ymixT = carve(38, 8, BF16, [KC, T])
    ymix_b = [abuf(38 + c, 1) for c in range(KC)]
    qmT = carve(0, 8, BF16, [KC, T]); qmT_b = [abuf(c, 1) for c in range(KC)]
    PmT = [carve(8 + 2 * i, 2, BF16, [2, T]) for i in range(2)]; PmT_b = [abuf(8 + 2 * i, 2) for i in range(2)]
    rec = [carve(12 + 2 * i, 2, F32, [T]) for i in range(2)]; rec_b = [abuf(12 + 2 * i, 2) for i in range(2)]
    omT = carve(16, 8, BF16, [KC, T]); omT_b = [abuf(16 + c, 1) for c in range(KC)]
    actT = carve(0, 22, BF16, [NFC, T]); actT_b = [abuf(c, 1) for c in range(NFC)]
    thf = [carve(22 + 2 * i, 2, F32, [T]) for i in range(2)]; thf_b = [abuf(22 + 2 * i, 2) for i in range(2)]
    wvf = [carve(26 + 2 * i, 2, F32, [T]) for i in range(2)]; wvf_b = [abuf(26 + 2 * i, 2) for i in range(2)]

    NBANK = 8
    banks = [nc.alloc_psum_tensor("bank%d" % i, [P, 512], F32) for i in range(NBANK)]
    bank_b = [Buf(excl=True) for _ in range(NBANK)]
    bank_ctr = [0]

    def next_bank():
        i = bank_ctr[0] % NBANK
        bank_ctr[0] += 1
        return banks[i], bank_b[i]

    def dump(name, ap, rd):
        if name not in debug_names:
            return
        shp = list(ap.shape)
        t = nc.dram_tensor("dbg_" + name, shp, ap.dtype, kind="ExternalOutput").ap()
        ds = DmaSem(nc, "dbg_" + name)
        sc.op("sp", lambda: nc.sync.dma_start(out=t, in_=ap), reads=rd, writes=[], dma=ds)
        dbg[name] = ds

    ws_ctr = [0]

    def wload(name):
        if name in conv_order:
            ensure_conv(conv_order.index(name) + 5)
        i = ws_ctr[0] % NSLOT
        ws_ctr[0] += 1
        ln = 2048 if name in ("kv", "wd5") else (31 * P if name.startswith("dg") else UNIT)
        src = scr[uidx[name]][:, 0:ln]
        sc.op("sp", lambda: nc.sync.dma_start(out=wslot[i][:, 0:ln], in_=src),
              reads=[scr_buf[name]], writes=[wslot_b[i]], dma=wslot_sem[i])
        return wslot[i], wslot_b[i]

    csem = DmaSem(nc, "consts")
    csem2 = DmaSem(nc, "consts2")

    def emit_consts():
        sc.op("pool", lambda: nc.gpsimd.dma_start(out=ident[:], in_=ident_d), writes=[], dma=csem2)
        sc.op("pool", lambda: nc.gpsimd.dma_start(out=masks[:].rearrange("p a b -> p (a b)"), in_=mask_d), writes=[], dma=csem2)
        me = ("d", csem2, csem2.count)
        for b in (ident_b, masks_b):
            b.atoms[0].w = me
        sc.op("dve", lambda: nc.vector.memset(bqkv_pad[:], 0.0), writes=[bqkv_b])
        sc.op("dve", lambda: nc.vector.memset(bout_pad[:], 0.0), writes=[bout_b])
        sc.op("dve", lambda: nc.vector.memset(ones_pad[:], 0.0), writes=[ones_pad_b])
        sc.op("dve", lambda: nc.vector.memset(ones_full[:], 1.0), writes=[ones_full_b])
        sc.op("dve", lambda: nc.vector.memset(mhalf[:], -0.5), writes=[mhalf_b])
        sc.op("dve", lambda: nc.vector.memset(epsc[:], EPS), writes=[mhalf_b])
        sc.op("dve", lambda: nc.vector.memset(ones_pad[0:1, :], 1.0), writes=[ones_pad_b])
        bq = DmaSem(nc, "bq")
        sc.op("pool", lambda: nc.gpsimd.dma_start(out=bqkv_pad[0:1, :], in_=b_in_d[1024:1792].unsqueeze(0)),
              writes=[bqkv_b], dma=bq)
        bo = DmaSem(nc, "bo")
        sc.op("pool", lambda: nc.gpsimd.dma_start(out=bout_pad[0:1, :], in_=b_out_d.unsqueeze(0)),
              writes=[bout_b], dma=bo)
        sc.op("dve", lambda: nc.vector.memset(xld[0][:], 0.0), writes=[xld_b[0]])
        sc.op("dve", lambda: nc.vector.memset(xld[1][:], 0.0), writes=[xld_b[1]])
        st_sem = [DmaSem(nc, "stg0"), DmaSem(nc, "stg1")]
        def rload(k, dst, src):
            sc.op("sp", lambda: nc.sync.dma_start(out=dst, in_=src), reads=[xld_b[k]], writes=[], dma=st_sem[k])
        rload(0, xld[0][0:31, 0:512], w_dw_d)
        rload(0, xld[0][31:32, 0:512], b_dw_d.unsqueeze(0))
        rload(0, xld[0][32:33, 0:512], g_ln_d.unsqueeze(0))
        rload(0, xld[0][33:34, 0:512], b_ln_d.unsqueeze(0))
        rload(0, xld[0][34:35, 0:512], b_in_d[0:512].unsqueeze(0))
        rload(0, xld[0][35:36, 0:512], b_in_d[512:1024].unsqueeze(0))
        for k, g in enumerate((g_mix_d, g_mq_d, g_ffn_d, g_mkv_d)):
            rload(1, xld[1][k:k + 1, :], g.unsqueeze(0))
        sc.op("sp", lambda: nc.sync.dma_start(out=identf[:], in_=ident_d), writes=[], dma=csem)
        sc.op("sp", lambda: nc.sync.dma_start(out=gfin[:], in_=g_fin_d.partition_broadcast(P)), writes=[], dma=csem)
        sc.op("sp", lambda: nc.sync.dma_start(out=sinkt[:], in_=sink_d.partition_broadcast(P)), writes=[], dma=csem)
        me = ("d", csem, csem.count)
        cst0_b.atoms[0].w = me
        for k in range(2):
            me = ("d", st_sem[k], st_sem[k].count)
            for a in xld_b[k].atoms:
                a.w = me
        bk1, bk1b = next_bank()
        def tv1():
            ins = None
            for c in range(4):
                ins = nc.tensor.matmul(bk1[:, c * 36:(c + 1) * 36], xld[0][:, c * P:(c + 1) * P], identf[:, 0:36], start=True, stop=True)
            return ins
        sc.op("pe", tv1, reads=[xld_b[0], cst0_b], writes=[bk1b])
        bk2, bk2b = next_bank()
        def tv2():
            ins = None
            for kc in range(KC):
                ins = nc.tensor.matmul(bk2[:, kc * 4:(kc + 1) * 4], xld[1][:, kc * P:(kc + 1) * P], identf[:, 0:4], start=True, stop=True)
            return ins
        sc.op("pe", tv2, reads=[xld_b[1], cst0_b], writes=[bk2b])
        sc.op("dve", lambda: nc.vector.tensor_copy(vec1[:, :, 0:36], bk1[:, 0:144].rearrange("p (c r) -> p c r", c=4)),
              reads=[bk1b], writes=[consts_b])
        sc.op("dve", lambda: nc.vector.tensor_copy(vec2[:], bk2[:, 0:32].rearrange("p (c r) -> p c r", c=KC)),
              reads=[bk2b, cst0_b], writes=[consts_b])
        sc.op("dve", lambda: nc.vector.tensor_scalar(vec1[:, :, 36:38], vec1[:, :, 34:36], 0.5, None, ALU.mult),
              reads=[consts_b], writes=[vecd_b])
        sc.op("dve", lambda: nc.vector.tensor_scalar(vec1[:, :, 38:40], vec1[:, :, 32:34], 0.5, None, ALU.mult),
              reads=[consts_b], writes=[vecd_b])
        sc.op("act", lambda: nc.scalar.activation(expsink[:], sinkt[:], AF.Exp), reads=[cst0_b], writes=[expsink_b])

    norm_ctr = [0]
    junk = sb("junk", [P, D], BF16); junk_b = Buf()
    statg = [sb("statg%d" % i, [P, 12], F32) for i in range(3)]
    statg_b = [[bufs(4) for _ in range(3)] for _ in range(3)]
    ng_ctr = [0]

    def next_stat():
        k = ng_ctr[0] % 3
        ng_ctr[0] += 1
        return statg[k], statg_b[k]

    def norm_sq(src_ap, src_b, st, sb_, j):
        sc.op("act", lambda: nc.scalar.activation(junk[:], src_ap, AF.Square, accum_out=st[:, j:j + 1]),
              reads=[src_b], writes=[junk_b, sb_[0][j]])

    def norm_rstd(st, sb_, j0, n):
        sc.op("dve", lambda: nc.vector.tensor_scalar(st[:, 4 + j0:4 + j0 + n], st[:, j0:j0 + n], 1.0 / D, EPS, ALU.mult, ALU.add),
              reads=sb_[0][j0:j0 + n], writes=sb_[1][j0:j0 + n])
        sc.op("pool", lambda: nc.gpsimd.tensor_tensor(st[:, 8 + j0:8 + j0 + n], st[:, 4 + j0:4 + j0 + n], mhalf[:, 0:n], ALU.pow),
              reads=sb_[1][j0:j0 + n] + [mhalf_b], writes=sb_[2][j0:j0 + n])

    def norm_apply(src_ap, src_b, st, sb_, j, gcol, hT_i, tcol):
        s = norm_ctr[0] % 2
        norm_ctr[0] += 1
        sc.op("act", lambda: nc.scalar.activation(xnb[s][:], src_ap, AF.Copy, scale=st[:, 8 + j:9 + j]),
              reads=[src_b, sb_[2][j]], writes=[xnb_b[s]])
        bk, bkb = next_bank()
        pst = bk[:].bitcast(BF16).rearrange("p (k n) -> p k n", k=KC)

        def tr():
            ins = None
            for kc in range(KC):
                ins = nc.tensor.transpose(pst[:, kc, :], xnb[s][:, kc * P:(kc + 1) * P], ident[:])
            return ins
        sc.op("pe", tr, reads=[xnb_b[s], ident_b], writes=[bkb])
        dst = hT[hT_i][:, :, tcol * P:(tcol + 1) * P]
        gb = vec2[:, :, gcol:gcol + 1].to_broadcast([P, KC, P])
        sc.op("dve", lambda: nc.vector.tensor_tensor(dst, pst, gb, ALU.mult),
              reads=[bkb, consts_b], writes=[hT_b[hT_i][tcol]])

    def norm_group_pre(srcs):
        st, sb_ = next_stat()
        n = len(srcs)
        for j, (ap, b) in enumerate(srcs):
            norm_sq(ap, b, st, sb_, j)
        norm_rstd(st, sb_, 0, n)
        return (st, sb_, srcs)

    def norm_group_post(ctx, gcol, hT_i):
        st, sb_, srcs = ctx
        for j, (ap, b) in enumerate(srcs):
            norm_apply(ap, b, st, sb_, j, gcol, hT_i, j)

    def norm_group(srcs, gcol, hT_i):
        norm_group_post(norm_group_pre(srcs), gcol, hT_i)

    def mm_group(out_ap, pairs, start_first=True):
        def f():
            ins = None
            n = len(pairs)
            for k, (l, r) in enumerate(pairs):
                ins = nc.tensor.matmul(out_ap, l, r, start=(k == 0 and start_first), stop=(k == n - 1))
            return ins
        return f

    HA, HM, HF = 0, 1, 2

    def a1_xload(i, t):
        b = 4 * i + t
        xs = t % 2
        sc.op("sp", lambda: nc.sync.dma_start(out=xld[xs][:], in_=x_d[b * P:(b + 1) * P, :]),
              writes=[xld_b[xs]], dma=xld_sem[xs])

    def stage_A1_pre(i):
        rs = i % 2
        sc.op("sp", lambda: nc.sync.dma_start(out=ropec[rs][:], in_=ropec_d[i * T:(i + 1) * T, :].rearrange("(t p) c -> p t c", p=P)),
              writes=[rope_b[rs]], dma=rope_sem[rs])
        sc.op("sp", lambda: nc.sync.dma_start(out=ropes[rs][:], in_=ropes_d[i * T:(i + 1) * T, :].rearrange("(t p) c -> p t c", p=P)),
              writes=[rope_b[rs]], dma=rope_sem[rs])
        a1_xload(i, 0)
        a1_xload(i, 1)

    a1_state = {}

    def stage_A1(i, part=2):
        gslot = i % 2
        kslot = i % 3
        h_i = HA
        rs = i % 2
        if part in (0, 2):
            st, sb_ = next_stat()
            wq_ap, wq_b = wload("q")
            wkv_ap, wkv_b = wload("kv")
        else:
            st, sb_, wq_ap, wq_b, wkv_ap, wkv_b = a1_state[i]
        wq3 = wq_ap[:, 0:KC * 512].rearrange("p (kc n) -> p kc n", kc=KC)
        wkv3 = wkv_ap[:, 0:KC * 256].rearrange("p (kc n) -> p kc n", kc=KC)

        def qkv(t):
            ts_ = slice(t * P, (t + 1) * P)
            qs = t % 2
            bk, bkb = next_bank()
            pairs = [(ones_pad[:], bqkv_pad[:, 0:512])] + [(hT[h_i][:, kc, ts_], wq3[:, kc, :]) for kc in range(KC)]
            sc.op("pe", mm_group(bk[:], pairs), reads=[ones_pad_b, bqkv_b, hT_b[h_i][t], wq_b], writes=[bkb])
            bkk, bkkb = next_bank()
            pairs = [(ones_pad[:], bqkv_pad[:, 512:768])] + [(hT[h_i][:, kc, ts_], wkv3[:, kc, :]) for kc in range(KC)]
            sc.op("pe", mm_group(bkk[:, 0:256], pairs), reads=[ones_pad_b, bqkv_b, hT_b[h_i][t], wkv_b], writes=[bkkb])
            q3 = bk[:].rearrange("p (h d) -> p h d", h=8)
            cc = ropec[rs][:, t, :]
            ss = ropes[rs][:, t, :]
            a3 = rtA[:, 0:512].rearrange("p (h d) -> p h d", h=8)
            b3 = rtB[:, 0:512].rearrange("p (h d) -> p h d", h=8)
            sc.op("dve", lambda: nc.vector.tensor_tensor(a3, q3, cc.unsqueeze(1).to_broadcast([P, 8, 64]), ALU.mult),
                  reads=[bkb, rope_b[rs]], writes=[rtA_b])
            sc.op("dve", lambda: nc.vector.tensor_tensor(b3[:, :, 0:32], q3[:, :, 32:64], ss[:, 0:32].unsqueeze(1).to_broadcast([P, 8, 32]), ALU.mult),
                  reads=[bkb, rope_b[rs]], writes=[rtB_b])
            sc.op("dve", lambda: nc.vector.tensor_tensor(b3[:, :, 32:64], q3[:, :, 0:32], ss[:, 32:64].unsqueeze(1).to_broadcast([P, 8, 32]), ALU.mult),
                  reads=[bkb, rope_b[rs]], writes=[rtB_b])
            sc.op("dve", lambda: nc.vector.tensor_tensor(qrot[qs][:], rtA[:, 0:512], rtB[:, 0:512], ALU.add),
                  reads=[rtA_b, rtB_b], writes=[qrot_b[qs]])
            k3 = bkk[:, 0:128].rearrange("p (h d) -> p h d", h=2)
            v3 = bkk[:, 128:256].rearrange("p (h d) -> p h d", h=2)
            ak = rtA[:, 512:640].rearrange("p (h d) -> p h d", h=2)
            bk_ = rtB[:, 512:640].rearrange("p (h d) -> p h d", h=2)
            sc.op("dve", lambda: nc.vector.tensor_tensor(ak, k3, cc.unsqueeze(1).to_broadcast([P, 2, 64]), ALU.mult),
                  reads=[bkkb, rope_b[rs]], writes=[rtA_b])
            sc.op("dve", lambda: nc.vector.tensor_tensor(bk_[:, :, 0:32], k3[:, :, 32:64], ss[:, 0:32].unsqueeze(1).to_broadcast([P, 2, 32]), ALU.mult),
                  reads=[bkkb, rope_b[rs]], writes=[rtB_b])
            sc.op("dve", lambda: nc.vector.tensor_tensor(bk_[:, :, 32:64], k3[:, :, 0:32], ss[:, 32:64].unsqueeze(1).to_broadcast([P, 2, 32]), ALU.mult),
                  reads=[bkkb, rope_b[rs]], writes=[rtB_b])
            kp = kpad[qs][:].rearrange("p (g v) n -> p g v n", g=2)
            sc.op("dve", lambda: nc.vector.tensor_tensor(kp[:, :, 0, 0:64], ak, bk_, ALU.add),
                  reads=[rtA_b, rtB_b], writes=[kpad_b[qs]])
            sc.op("dve", lambda: nc.vector.tensor_tensor(kp[:, :, 1, 64:128], ak, bk_, ALU.add),
                  reads=[rtA_b, rtB_b], writes=[kpad_b[qs]])
            sc.op("dve", lambda: nc.vector.tensor_copy(Vr[kslot][:, t, :, 0:64], v3),
                  reads=[bkkb], writes=[Vr_b[kslot][t]])

        def rot_t(t):
            ts_ = slice(t * P, (t + 1) * P)
            qs = t % 2
            bk2, bkb2 = next_bank()
            pq = bk2[:].bitcast(BF16)[:, 0:512].rearrange("p (k n) -> p k n", k=4)

            def trq():
                ins = None
                for j in range(4):
                    ins = nc.tensor.transpose(pq[:, j, :], qrot[qs][:, j * P:(j + 1) * P], ident[:])
                return ins
            sc.op("pe", trq, reads=[qrot_b[qs], ident_b], writes=[bkb2])
            sc.op("act", lambda: nc.scalar.copy(qT[gslot][:, :, ts_], pq), reads=[bkb2], writes=[qT_b[gslot][t]])
            bk3, bkb3 = next_bank()
            pk = bk3[:].bitcast(BF16)[:, 0:512].rearrange("p (k n) -> p k n", k=4)

            def trk():
                ins = None
                for j in range(4):
                    ins = nc.tensor.transpose(pk[:, j, :], kpad[qs][:, j, :], ident[:])
                return ins
            sc.op("pe", trk, reads=[kpad_b[qs], ident_b], writes=[bkb3])
            sc.op("act", lambda: nc.scalar.copy(kTz[kslot][:, :, ts_], pk), reads=[bkb3], writes=[kTz_b[kslot][t]])

        if part in (0, 2):
            for t in range(2):
                norm_sq(xld[t][:], xld_b[t], st, sb_, t)
            norm_rstd(st, sb_, 0, 2)
            for t in range(2):
                norm_apply(xld[t][:], xld_b[t], st, sb_, t, 0, h_i, t)
            a1_xload(i, 2)
            a1_xload(i, 3)
            for t in range(2, 4):
                norm_sq(xld[t % 2][:], xld_b[t % 2], st, sb_, t)
            norm_rstd(st, sb_, 2, 2)
            a1_state[i] = (st, sb_, wq_ap, wq_b, wkv_ap, wkv_b)
        if part == 0:
            return
        qkv(0)
        qkv(1)
        for t in range(2, 4):
            norm_apply(xld[t % 2][:], xld_b[t % 2], st, sb_, t, 0, h_i, t)
        rot_t(0)
        rot_t(1)
        qkv(2)
        qkv(3)
        rot_t(2)
        rot_t(3)

    def stage_A2(i, part=2):
        gslot = i % 2
        h_i = HA
        for u in ((0, 1) if part == 2 else (part,)):
            w_ap, w_b = wload("glu%d" % u)
            w3 = w_ap[:, 0:KC * 512].rearrange("p (kc n) -> p kc n", kc=KC)
            for j in range(2):
                c = 2 * u + j
                ts2 = c % 2
                bka, bkab = next_bank()
                sc.op("pe", mm_group(bka[:], [(w3[:, kc, j * P:(j + 1) * P], hT[h_i][:, kc, :]) for kc in range(KC)]),
                      reads=[w_b] + hT_b[h_i], writes=[bkab])
                bkg, bkgb = next_bank()
                sc.op("pe", mm_group(bkg[:], [(w3[:, kc, 256 + j * P:256 + (j + 1) * P], hT[h_i][:, kc, :]) for kc in range(KC)]),
                      reads=[w_b] + hT_b[h_i], writes=[bkgb])
                sc.op("act", lambda bkg=bkg, c=c, ts2=ts2: nc.scalar.activation(tgA[ts2][:], bkg[:], AF.Tanh, bias=vec1[:, c, 37:38], scale=0.5),
                      reads=[bkgb, vecd_b], writes=[tgA_b[ts2]])
                sc.op("act", lambda bka=bka, c=c, ts2=ts2: nc.scalar.activation(ahA[ts2][:], bka[:], AF.Identity, bias=vec1[:, c, 36:37], scale=0.5),
                      reads=[bkab, vecd_b], writes=[ahA_b[ts2]])
                sc.op("dve", lambda c=c, ts2=ts2: nc.vector.scalar_tensor_tensor(gl[gslot][:, c, 16:16 + T], tgA[ts2][:], 1.0, ahA[ts2][:], ALU.add, ALU.mult),
                      reads=[tgA_b[ts2], ahA_b[ts2]], writes=[gl_c[gslot][c]])
        if part == 0:
            return
        if i == 0:
            sc.op("dve", lambda: nc.vector.memset(gl[gslot][:, :, 0:16], 0.0), writes=[gl_lh[gslot]])
        else:
            ps_ = (i - 1) % 2
            sc.op("dve", lambda: nc.vector.tensor_copy(gl[gslot][:, :, 0:16], gl[ps_][:, :, T:T + 16]),
                  reads=gl_c[ps_], writes=[gl_lh[gslot]])
            sc.op("dve", lambda: nc.vector.tensor_copy(gl[ps_][:, :, 16 + T:32 + T], gl[gslot][:, :, 16:32]),
                  reads=gl_c[gslot], writes=[gl_rh[ps_]])
        if i == NG - 1:
            sc.op("dve", lambda: nc.vector.memset(gl[gslot][:, :, 16 + T:32 + T], 0.0), writes=[gl_rh[gslot]])

    def mem_prologue():
        h_i = HM
        for t in range(2):
            sc.op("sp", lambda t=t: nc.sync.dma_start(out=xld[t][:], in_=mem_d[t * P:(t + 1) * P, :]),
                  writes=[xld_b[t]], dma=xld_sem[t])
        norm_group([(xld[t][:], xld_b[t]) for t in range(2)], 3, h_i)
        for u in range(2):
            w_ap, w_b = wload("kvm%d" % u)
            w3 = w_ap[:, 0:KC * 512].rearrange("p (kc n) -> p kc n", kc=KC)
            for j in range(4):
                oc = 4 * u + j
                bk, bkb = next_bank()
                sc.op("pe", mm_group(bk[:, 0:MEM], [(w3[:, kc, j * P:(j + 1) * P], hT[h_i][:, kc, 0:MEM]) for kc in range(KC)]),
                      reads=[w_b, hT_b[h_i][0], hT_b[h_i][1]], writes=[bkb])
                sc.op("act", lambda bk=bk, oc=oc: nc.scalar.copy(KmT[:, oc, :], bk[:, 0:MEM]),
                      reads=[bkb], writes=[KmT_b[oc]])
        for u in range(2):
            w_ap, w_b = wload("kvm%d" % (2 + u))
            w3 = w_ap[:, 0:KC * 512].rearrange("p (kc n) -> p kc n", kc=KC)
            for mt in range(2):
                bk, bkb = next_bank()
                sc.op("pe", mm_group(bk[:], [(hT[h_i][:, kc, mt * P:(mt + 1) * P], w3[:, kc, :]) for kc in range(KC)]),
                      reads=[w_b, hT_b[h_i][mt]], writes=[bkb])
                sc.op("act", lambda bk=bk, mt=mt, u=u: nc.scalar.copy(Vm[:, mt, u * 512:(u + 1) * 512], bk[:]),
                      reads=[bkb], writes=[Vm_b[mt][u]])

    def mixer(i, pre_out=None, mid=None):
        gslot = i % 2
        def conv_chunk(c):
            w_ap, w_b = wload("dg%d" % c)
            dg3 = w_ap[:, 0:31 * P].rearrange("p (j n) -> p j n", j=31)
            bk, bkb = next_bank()
            sc.op("pe", mm_group(bk[:], [(dg3[:, j, :], gl[gslot][:, c, j + 1:j + 1 + T]) for j in range(31)]),
                  reads=[w_b, gl_c[gslot][c], gl_lh[gslot], gl_rh[gslot]], writes=[bkb])
            sc.op("dve", lambda: nc.vector.tensor_scalar(acc[:, c, :], bk[:], vec1[:, c, 31:32], None, ALU.add),
                  reads=[bkb, consts_b], writes=[acc_b[c]])
        ln_steps = []

        def ln1():
            if i == 0:
                ln_cast()
            bks, bksb = next_bank()
            sc.op("pe", mm_group(bks[:], [(ones_full[:], ybf[:, c, :]) for c in range(4)]), reads=[ones_full_b] + ybf_b, writes=[bksb])
            bkq, bkqb = next_bank()
            sc.op("pe", mm_group(bkq[:], [(ones_full[:], ysq[:, c, :]) for c in range(4)]), reads=[ones_full_b] + ysq_b, writes=[bkqb])
            ln_state["bks"] = (bks, bksb)
            ln_state["bkq"] = (bkq, bkqb)

        def ln2():
            bks, bksb = ln_state["bks"]
            bkq, bkqb = ln_state["bkq"]
            sc.op("dve", lambda: nc.vector.tensor_scalar(mean, bks[:], 1.0 / 512, None, ALU.mult), reads=[bksb], writes=[mean_b])
            sc.op("dve", lambda: nc.vector.tensor_tensor(var, mean, mean, ALU.mult), reads=[mean_b], writes=[var_b])
            sc.op("dve", lambda: nc.vector.scalar_tensor_tensor(var, bkq[:], 1.0 / 512, var, ALU.mult, ALU.subtract), reads=[bkqb, var_b], writes=[var_b])
            sc.op("act", lambda: nc.scalar.activation(var, var, AF.Sqrt, bias=epsc[:, 0:1]), reads=[var_b, mhalf_b], writes=[var_b])

        def ln3():
            sc.op("dve", lambda: nc.vector.reciprocal(rstd, var), reads=[var_b], writes=[rstd_b])

        def ln_chunk(c):
            def f():
                sc.op("dve", lambda: nc.vector.tensor_tensor(dtm, acc[:, c, :], mean, ALU.subtract), reads=[acc_b[c], mean_b], writes=[dtm_b])
                sc.op("dve", lambda: nc.vector.tensor_tensor(dtm, dtm, rstd, ALU.mult), reads=[dtm_b, rstd_b], writes=[dtm_b])
                sc.op("act", lambda: nc.scalar.activation(thm, dtm, AF.Tanh, bias=vec1[:, c, 39:40], scale=vec1[:, c, 38:39]),
                      reads=[dtm_b, vecd_b], writes=[thm_b])
                sc.op("dve", lambda: nc.vector.tensor_scalar(zhm, dtm, vec1[:, c, 38:39], vec1[:, c, 39:40], ALU.mult, ALU.add),
                      reads=[dtm_b, vecd_b], writes=[zhm_b])
                sc.op("dve", lambda: nc.vector.scalar_tensor_tensor(ymixT[:, c, :], thm, 1.0, zhm, ALU.add, ALU.mult),
                      reads=[thm_b, zhm_b], writes=[ymix_b[c]])
            return f
        ln_state = {}
        ln_steps = [ln1, ln2, ln3] + [ln_chunk(c) for c in range(4)]
        kbs = [kb for kb in range(4 * i - 1, 4 * i + 5) if 0 <= kb < NB]
        tiles = []
        for kb in kbs:
            qlo = max(kb - 1, 4 * i)
            qhi = min(kb + 1, 4 * i + 3)
            tiles.append((kb, qlo, qhi, (qhi - qlo + 1) * P))
        binfill = []
        place = {}
        for (kb, qlo, qhi, n) in sorted(tiles, key=lambda x: -x[3]):
            for bi in range(len(binfill)):
                if binfill[bi] + n <= 512:
                    place[kb] = (bi, binfill[bi])
                    binfill[bi] += n
                    break
            else:
                place[kb] = (len(binfill), 0)
                binfill.append(n)
        col0 = {}
        for (kb, qlo, qhi, n) in tiles:
            bi, off = place[kb]
            for qb in range(qlo, qhi + 1):
                col0[(kb, qb)] = bi * 512 + off + (qb - qlo) * P

        def scores(h):
            g = h // 4
            pr = h // 2
            va = h % 2
            pb_i = h % 2
            PTh = PT[pb_i]
            sbanks = [next_bank() for _ in binfill]
            for (kb, qlo, qhi, n) in tiles:
                bi, off = place[kb]
                bk, bkb = sbanks[bi]
                ksl = (kb // 4) % 3
                kt = kb % 4
                lhsT = kTz[ksl][:, g * 2 + va, kt * P:(kt + 1) * P]
                rhs = qT[gslot][:, pr, (qlo - 4 * i) * P:(qhi - 4 * i + 1) * P]
                side = []
                for qb in range(qlo, qhi + 1):
                    if qb == kb + 1:
                        side.append((off + (qb - qlo) * P, 0))
                    elif qb == kb - 1:
                        side.append((off + (qb - qlo) * P, 1))

                def smm(bk=bk, off=off, n=n, lhsT=lhsT, rhs=rhs, side=side):
                    ins = nc.tensor.matmul(bk[:, off:off + n], lhsT, rhs, start=True, stop=(len(side) == 0))
                    for k_, (co, wm) in enumerate(side):
                        ins = nc.tensor.matmul(bk[:, co:co + P], ident[:], masks[:, wm, :], start=False, stop=(k_ == len(side) - 1))
                    return ins
                sc.op("pe", smm,
                      reads=[kTz_b[ksl][kt], ident_b, masks_b] + [qT_b[gslot][q - 4 * i] for q in range(qlo, qhi + 1)], writes=[bkb])
            for bi, fill in enumerate(binfill):
                bk, bkb = sbanks[bi]
                sc.op("act", lambda bk=bk, bi=bi, fill=fill, PTh=PTh: nc.scalar.activation(PTh[:, bi * 512:bi * 512 + fill], bk[:, 0:fill], AF.Exp, scale=0.125),
                      reads=[bkb], writes=[PT_b[pb_i]])

        def pvout(h):
            g = h // 4
            pb_i = h % 2
            PTh = PT[pb_i]
            bko, bkob = next_bank()
            po = bko[:, 0:260].rearrange("p (q d) -> p q d", q=4)

            def pv():
                ins = None
                for ql in range(4):
                    qb = 4 * i + ql
                    kl = [kb for kb in (qb - 1, qb, qb + 1) if 0 <= kb < NB]
                    for n_, kb in enumerate(kl):
                        cq = col0[(kb, qb)]
                        ins = nc.tensor.matmul(po[:, ql, :], PTh[:, cq:cq + P], Vr[(kb // 4) % 3][:, kb % 4, g, :],
                                               start=(n_ == 0), stop=(n_ == len(kl) - 1))
                return ins
            vrd = [Vr_b[(kb // 4) % 3][kb % 4] for kb in kbs]
            sc.op("pe", pv, reads=[PT_b[pb_i]] + vrd, writes=[bkob])
            dn = dens[h % 2]
            dnb = dens_b[h % 2]
            sc.op("dve", lambda: nc.vector.tensor_scalar(dn[:, 0:4], po[:, :, 64], expsink[:, h:h + 1], None, ALU.add),
                  reads=[bkob, expsink_b], writes=[dnb])
            sc.op("dve", lambda: nc.vector.reciprocal(dn[:, 4:8], dn[:, 0:4]), reads=[dnb], writes=[dnb])
            ya = yatt[:, :, h * 64:(h + 1) * 64]
            sc.op("dve", lambda: nc.vector.tensor_tensor(ya, po[:, :, 0:64], dn[:, 4:8].unsqueeze(2).to_broadcast([P, 4, 64]), ALU.mult),
                  reads=[bkob, dnb], writes=yatt_b)

        if i == 0:
            for c in range(4):
                conv_chunk(c)
        scores(0)
        scores(1)
        yield
        for h in range(8):
            if 1 <= h and h + 1 < 8:
                scores(h + 1)
            if h < len(ln_steps):
                ln_steps[h]()
            pvout(h)
            if 2 <= h <= 5 and mid is not None:
                mid(h - 2)
        for st_ in ln_steps[8:]:
            st_()
        if pre_out is not None:
            pre_out()
        for ql in range(4):
            bk, bkb = next_bank()
            pt_ = bk[:].bitcast(BF16)[:, 0:512].rearrange("p (k n) -> p k n", k=4)

            def tra(pt_=pt_, ql=ql):
                ins = None
                for j in range(4):
                    ins = nc.tensor.transpose(pt_[:, j, :], yatt[:, ql, j * P:(j + 1) * P], ident[:])
                return ins
            sc.op("pe", tra, reads=[yatt_b[ql], ident_b], writes=[bkb])
            sc.op("dve", lambda pt_=pt_, ql=ql: nc.vector.tensor_copy(ymixT[:, 4:8, ql * P:(ql + 1) * P], pt_),
                  reads=[bkb], writes=ymix_b[4:8])
        wl = [wload("out%d" % u) for u in range(2)]
        for t in range(4):
            b = 4 * i + t
            sc.op("sp", lambda t=t, b=b: nc.sync.dma_start(out=xres[:, t, :], in_=x_d[b * P:(b + 1) * P, :]),
                  writes=[xres_b[t]], dma=xres_sem[t])
        for t in range(4):
            for u in range(2):
                w_ap, w_b = wl[u]
                w3 = w_ap[:, 0:KC * 512].rearrange("p (kc n) -> p kc n", kc=KC)
                bk, bkb = next_bank()
                pairs = [(ones_pad[:], bout_pad[:, u * 512:(u + 1) * 512])] + [(ymixT[:, kc, t * P:(t + 1) * P], w3[:, kc, :]) for kc in range(KC)]
                sc.op("pe", mm_group(bk[:], pairs), reads=[ones_pad_b, bout_b, w_b] + ymix_b, writes=[bkb])
                xs_ = xres[:, t, u * 512:(u + 1) * 512]
                sc.op("dve", lambda bk=bk, xs_=xs_: nc.vector.tensor_tensor(xs_, bk[:], xs_, ALU.add),
                      reads=[bkb, xres_b[t]], writes=[xres_b[t]])
        dump("x1_%d" % i, xres[:], xres_b)

    def memattn_pre(i):
        return norm_group_pre([(xres[:, t, :], xres_b[t]) for t in range(4)])

    def memattn(i, ctx, filler=None):
        h_i = HM
        norm_group_post(ctx, 1, h_i)
        for u in range(2):
            w_ap, w_b = wload("wq%d" % u)
            w3 = w_ap[:, 0:KC * 512].rearrange("p (kc n) -> p kc n", kc=KC)
            for j in range(4):
                oc = 4 * u + j
                bk, bkb = next_bank()
                sc.op("pe", mm_group(bk[:], [(w3[:, kc, j * P:(j + 1) * P], hT[h_i][:, kc, :]) for kc in range(KC)]),
                      reads=[w_b] + hT_b[h_i], writes=[bkb])
                sc.op("act", lambda bk=bk, oc=oc: nc.scalar.copy(qmT[:, oc, :], bk[:]), reads=[bkb], writes=[qmT_b[oc]])
        def m_scores(hm):
            pi = hm % 2
            for mt in range(2):
                bk, bkb = next_bank()
                sc.op("pe", mm_group(bk[:], [(KmT[:, 2 * hm + dc, mt * P:(mt + 1) * P], qmT[:, 2 * hm + dc, :]) for dc in range(2)]),
                      reads=[KmT_b[2 * hm], KmT_b[2 * hm + 1], qmT_b[2 * hm], qmT_b[2 * hm + 1]], writes=[bkb])
                sc.op("act", lambda bk=bk, pi=pi, mt=mt: nc.scalar.activation(PmT[pi][:, mt, :], bk[:], AF.Exp, scale=1.0 / 16),
                      reads=[bkb], writes=[PmT_b[pi]])

        def m_pv(hm):
            pi = hm % 2
            bk, bkb = next_bank()
            sc.op("pe", mm_group(bk[:], [(ones_full[:], PmT[pi][:, mt, :]) for mt in range(2)]), reads=[ones_full_b, PmT_b[pi]], writes=[bkb])
            sc.op("dve", lambda bk=bk, pi=pi: nc.vector.reciprocal(rec[pi], bk[:]), reads=[bkb], writes=[rec_b[pi]])
            for dc in range(2):
                oc = 2 * hm + dc
                bk, bkb = next_bank()
                sc.op("pe", mm_group(bk[:], [(Vm[:, mt, oc * P:(oc + 1) * P], PmT[pi][:, mt, :]) for mt in range(2)]),
                      reads=[Vm_b[0][oc // 4], Vm_b[1][oc // 4], PmT_b[pi]], writes=[bkb])
                sc.op("dve", lambda bk=bk, oc=oc, pi=pi: nc.vector.tensor_tensor(omT[:, oc, :], bk[:], rec[pi], ALU.mult),
                      reads=[bkb, rec_b[pi]], writes=[omT_b[oc]])

        m_scores(0)
        for hm in range(4):
            if hm + 1 < 4:
                m_scores(hm + 1)
            m_pv(hm)
        if filler is not None:
            filler()
        wl = [wload("wo%d" % u) for u in range(2)]
        for t in range(4):
            for u in range(2):
                w_ap, w_b = wl[u]
                w3 = w_ap[:, 0:KC * 512].rearrange("p (kc n) -> p kc n", kc=KC)
                bk, bkb = next_bank()
                sc.op("pe", mm_group(bk[:], [(omT[:, kc, t * P:(t + 1) * P], w3[:, kc, :]) for kc in range(KC)]),
                      reads=[w_b] + omT_b, writes=[bkb])
                xs_ = xres[:, t, u * 512:(u + 1) * 512]
                sc.op("dve", lambda bk=bk, xs_=xs_: nc.vector.tensor_tensor(xs_, bk[:], xs_, ALU.add),
                      reads=[bkb, xres_b[t]], writes=[xres_b[t]])
        dump("x2_%d" % i, xres[:], xres_b)

    def ffn_pre(i):
        return norm_group_pre([(xres[:, t, :], xres_b[t]) for t in range(4)])

    def ln_cast():
        for c in range(4):
            sc.op("act", lambda c=c: nc.scalar.copy(ybf[:, c, :], acc[:, c, :]), reads=[acc_b[c]], writes=[ybf_b[c]])
            sc.op("pool", lambda c=c: nc.gpsimd.tensor_tensor(ysq[:, c, :], acc[:, c, :], acc[:, c, :], ALU.mult), reads=[acc_b[c]], writes=[ysq_b[c]])

    def conv_ops(g):
        gs = g % 2
        ops = []
        for j in range(31):
            for c in range(4):
                if j == 0:
                    def f(c=c):
                        sc.op("dve", lambda: nc.vector.tensor_scalar(acc[:, c, :], gl[gs][:, c, 1:1 + T], vec1[:, c, 0:1], vec1[:, c, 31:32], ALU.mult, ALU.add),
                              reads=[gl_c[gs][c], gl_lh[gs], consts_b], writes=[acc_b[c]])
                else:
                    def f(c=c, j=j):
                        rd = [gl_c[gs][c], acc_b[c]]
                        if j < 15:
                            rd.append(gl_lh[gs])
                        if j > 15:
                            rd.append(gl_rh[gs])
                        sc.op("dve", lambda: nc.vector.scalar_tensor_tensor(acc[:, c, :], gl[gs][:, c, j + 1:j + 1 + T], vec1[:, c, j:j + 1], acc[:, c, :], ALU.mult, ALU.add),
                              reads=rd, writes=[acc_b[c]])
                ops.append(f)
        return ops

    def ffn(i, ctx):
        h_i = HF
        norm_group_post(ctx, 2, h_i)
        cops = conv_ops(i + 1) if i + 1 < NG else []
        cpos = [0]

        def emit_conv(n):
            for _ in range(n):
                if cpos[0] < len(cops):
                    cops[cpos[0]]()
                    cpos[0] += 1
        for m in range(11):
            w_ap, w_b = wload("gu%d" % m)
            w4 = w_ap[:].rearrange("p (g kc n) -> p g kc n", g=2, kc=KC)
            for j in range(2):
                c = 2 * m + j
                fs = c % 2
                bkg, bkgb = next_bank()
                sc.op("pe", mm_group(bkg[:], [(w4[:, 0, kc, j * P:(j + 1) * P], hT[h_i][:, kc, :]) for kc in range(KC)]),
                      reads=[w_b] + hT_b[h_i], writes=[bkgb])
                bku, bkub = next_bank()
                sc.op("pe", mm_group(bku[:], [(w4[:, 1, kc, j * P:(j + 1) * P], hT[h_i][:, kc, :]) for kc in range(KC)]),
                      reads=[w_b] + hT_b[h_i], writes=[bkub])
                sc.op("act", lambda bkg=bkg, fs=fs: nc.scalar.activation(thf[fs], bkg[:], AF.Silu),
                      reads=[bkgb], writes=[thf_b[fs]])
                sc.op("dve", lambda bku=bku, fs=fs, c=c: nc.vector.tensor_tensor(actT[:, c, :], bku[:], thf[fs], ALU.mult),
                      reads=[thf_b[fs], bkub], writes=[actT_b[c]])
                emit_conv(4)
        emit_conv(len(cops))
        if i + 1 < NG:
            ln_cast()
        accb = {}
        for t in range(4):
            for u in range(2):
                accb[(t, u)] = next_bank()
        for m in range(6):
            w_ap, w_b = wload("wd%d" % m)
            nch = 4 if m < 5 else 2
            w3 = w_ap[:, 0:nch * D].rearrange("p (c n) -> p c n", c=nch)
            for cl in range(nch):
                c = 4 * m + cl
                for t in range(4):
                    for u in range(2):
                        bk, bkb = accb[(t, u)]

                        def f(bk=bk, c=c, t=t, u=u, w3=w3, cl=cl):
                            return nc.tensor.matmul(bk[:], actT[:, c, t * P:(t + 1) * P], w3[:, cl, u * 512:(u + 1) * 512],
                                                    start=(c == 0), stop=(c == NFC - 1))
                        sc.op("pe", f, reads=[actT_b[c], w_b], writes=[bkb])
        for t in range(4):
            for u in range(2):
                bk, bkb = accb[(t, u)]
                xs_ = xres[:, t, u * 512:(u + 1) * 512]
                sc.op("dve", lambda bk=bk, xs_=xs_: nc.vector.tensor_tensor(xs_, bk[:], xs_, ALU.add),
                      reads=[bkb, xres_b[t]], writes=[xres_b[t]])
        dump("x3_%d" % i, xres[:], xres_b)

    fin_state = {}

    def final_pre(i):
        st, sb_ = next_stat()
        for t in range(4):
            norm_sq(xres[:, t, :], xres_b[t], st, sb_, t)
        norm_rstd(st, sb_, 0, 4)
        fin_state[i] = (st, sb_)

    def final_post(i, tiles=(0, 1, 2, 3)):
        st, sb_ = fin_state[i]
        for t in tiles:
            b = 4 * i + t
            sc.op("dve", lambda t=t: nc.vector.scalar_tensor_tensor(xres[:, t, :], xres[:, t, :], st[:, 8 + t:9 + t], gfin[:], ALU.mult, ALU.mult),
                  reads=[xres_b[t], sb_[2][t], consts_b], writes=[xres_b[t]])
            sc.op("act", lambda t=t, b=b: nc.scalar.dma_start(out=out_d[b * P:(b + 1) * P, :], in_=xres[:, t, :]),
                  reads=[xres_b[t]], writes=[], dma=out_sem[t])
            for a_ in xres_b[t].atoms:
                a_.r.append(("d", out_sem[t], out_sem[t].count))

    dens = [sb("dens%d" % i, [P, 8], F32) for i in range(2)]
    dens_b = bufs(2)

    fast_state = {}

    def emit_fast_units(first):
        stg = [arena[:, k * 8192:(k + 1) * 8192].bitcast(F32) for k in range(2)]
        stg_b = [Buf(atoms[16 * k:16 * k + 16]) for k in range(2)]
        ubf = arena[:, 16384:16384 + UNIT]
        ubf_b = Buf(atoms[32:40])
        if first:
            fast_state["fsem"] = [DmaSem(nc, "fast_ld%d" % k) for k in range(2)]
            fast_state["ssem"] = DmaSem(nc, "fast_st")
            fast_state["stg_b"] = stg_b
            fast_state["ubf_b"] = ubf_b
        fsem = fast_state["fsem"]
        ssem = fast_state["ssem"]
        stg_b = fast_state["stg_b"]
        ubf_b = fast_state["ubf_b"]
        plan = [("q", [(1024, 1536, 0)], 512), ("kv", [(1536, 1792, 0)], 256),
                ("glu0", [(0, 256, 0), (512, 768, 256)], 512), ("glu1", [(256, 512, 0), (768, 1024, 256)], 512)]
        for k, (n, parts, w) in enumerate(plan):
            if (k < 2) != first:
                continue
            sl = k % 2
            s3 = stg[sl][:, 0:KC * w].rearrange("p (kc n) -> p kc n", kc=KC)
            for (c0, c1, d0) in parts:
                sc.op("sp", lambda s3=s3, c0=c0, c1=c1, d0=d0: nc.sync.dma_start(out=s3[:, :, d0:d0 + (c1 - c0)], in_=kcview(w_in_d, c0, c1)),
                      writes=[stg_b[sl]], dma=fsem[sl])
            sc.op("dve", lambda sl=sl, w=w: nc.vector.tensor_copy(ubf[:, 0:KC * w], stg[sl][:, 0:KC * w]),
                  reads=[stg_b[sl]], writes=[ubf_b])
            sc.op("sp", lambda n=n, w=w: nc.sync.dma_start(out=scr[uidx[n]][:, 0:KC * w], in_=ubf[:, 0:KC * w]),
                  reads=[ubf_b], writes=[], dma=ssem)
            me = ("d", ssem, ssem.count)
            for a in scr_buf[n].atoms:
                a.w = me
            for a in ubf_b.atoms:
                a.r.append(me)

    diag_state = {}

    def emit_diag_build():
        stg = arena[:, 0:4 * 31 * P].rearrange("p (c j n) -> p c j n", c=4, j=31)
        stg_b = Buf(atoms[0:31])
        for c in range(4):
            sc.op("dve", lambda c=c: nc.vector.tensor_tensor(
                stg[:, c], ident[:].unsqueeze(1).to_broadcast([P, 31, P]),
                vec1[:, c, 0:31].unsqueeze(2).to_broadcast([P, 31, P]), ALU.mult),
                reads=[ident_b, consts_b], writes=[stg_b])
        diag_state["stg"] = (stg, stg_b)

    def emit_diag_store():
        stg, stg_b = diag_state["stg"]
        for c in range(4):
            n = "dg%d" % c
            sc.op("sp", lambda c=c, n=n: nc.sync.dma_start(out=scr[uidx[n]][:, 0:31 * P], in_=stg[:, c].rearrange("p j n -> p (j n)")),
                  reads=[stg_b], writes=[], dma=scr_sem[n])
            me = ("d", scr_sem[n], scr_sem[n].count)
            for a in scr_buf[n].atoms:
                a.w = me
            for a in stg_b.atoms:
                a.r.append(me)

    emit_consts()
    stage_A1_pre(0)
    emit_fast_units(True)
    for k in range(3):
        sc.op("dve", lambda k=k: nc.vector.memset(Vr[k][:], 1.0), writes=Vr_b[k])
    for k in range(2):
        sc.op("dve", lambda k=k: nc.vector.memset(kpad[k][:], 0.0), writes=[kpad_b[k]])
    stage_A1(0)
    emit_fast_units(False)
    ensure_conv(5)
    emit_diag_build()
    if NG > 1:
        stage_A1_pre(1)
    stage_A2(0)
    emit_diag_store()
    if NG > 1:
        stage_A1(1)
        stage_A2(1)
    def mk_mixer(i):
        return mixer(i, pre_out=(lambda: stage_A1(i + 2, 0)) if i + 2 < NG else None,
                     mid=(lambda t: final_post(i - 1, (t,))) if i > 0 else None)

    g = mk_mixer(0)
    next(g)
    for i in range(NG):
        if i + 2 < NG:
            stage_A1_pre(i + 2)
        for _ in g:
            pass
        ctx = memattn_pre(i)
        if i + 2 < NG:
            stage_A1(i + 2, 1)
        if i == 0:
            mem_prologue()
        memattn(i, ctx, filler=(lambda i=i: stage_A2(i + 2, 0)) if i + 2 < NG else None)
        ctx = ffn_pre(i)
        if i + 2 < NG:
            stage_A2(i + 2, 1)
        ffn(i, ctx)
        if i + 1 < NG:
            g = mk_mixer(i + 1)
            next(g)
            final_pre(i)
        else:
            final_pre(i)
            final_post(i)
    sc.finalize()
    sc.final_wait("act", out_sem)
    sc.final_wait("sp", list(dbg.values()))
    return nc, sc


def _consts():
    pos = np.arange(S, dtype=np.float64)
    inv_freq = 10000.0 ** (-np.arange(0, 64, 2, dtype=np.float64) / 64.0)
    ang = pos[:, None] * inv_freq[None, :]
    c = np.cos(ang).astype(np.float32)
    s = np.sin(ang).astype(np.float32)
    cc = np.concatenate([c, c], axis=1)
    ss = np.concatenate([-s, s], axis=1)
    ident = np.eye(P, dtype=np.float32)
    cidx = np.arange(P)[:, None]
    aidx = np.arange(P)[None, :]
    m0 = np.where(cidx >= aidx, 0.0, -30000.0).astype(np.float32)
    m1 = np.where(cidx <= aidx, 0.0, -30000.0).astype(np.float32)
    mask = np.concatenate([m0, m1], axis=1)
    return dict(rope_cc=np.ascontiguousarray(cc), rope_ss=np.ascontiguousarray(ss), c_ident=ident,
                c_mask=np.ascontiguousarray(mask))


_CACHE = {}


def kernel(**inputs):
    if "prog" not in _CACHE:
        _CACHE["prog"] = build_program()
    nc, sc = _CACHE["prog"]
    f = lambda k: np.ascontiguousarray(np.asarray(inputs[k], dtype=np.float32))
    shared = dict(
        g_mix=f("g_mix").reshape(D), w_in=f("w_in").reshape(D, 1792), b_in=f("b_in").reshape(1792),
        w_dw=f("w_dw").reshape(31, 512), b_dw=f("b_dw").reshape(512), g_conv_ln=f("g_conv_ln").reshape(512),
        b_conv_ln=f("b_conv_ln").reshape(512), attn_sink=f("attn_sink").reshape(8),
        w_out=f("w_out").reshape(D, D), b_out=f("b_out").reshape(D), g_mem_q=f("g_mem_q").reshape(D),
        g_mem_kv=f("g_mem_kv").reshape(D), w_mem_q=f("w_mem_q").reshape(D, D),
        w_mem_kv=f("w_mem_kv").reshape(D, 2 * D), w_mem_o=f("w_mem_o").reshape(D, D),
        g_ffn=f("g_ffn").reshape(D), w_gate=f("w_gate").reshape(D, DFF), w_up=f("w_up").reshape(D, DFF),
        w_down=f("w_down").reshape(DFF, D), g_final=f("g_final").reshape(D),
    )
    shared.update(_consts())
    x = f("x")
    mem = f("mem")
    in_maps = []
    for c in range(8):
        m = dict(shared)
        m["x"] = np.ascontiguousarray(x[c])
        m["mem"] = np.ascontiguousarray(mem[c])
        in_maps.append(m)
    res = run_bass_kernel_spmd(nc, in_maps, core_ids=list(range(8)))
    out = np.stack([np.asarray(res.results[c]["out"], dtype=np.float32) for c in range(8)], axis=0)
    return out
```

```python
import numpy as np
import concourse.bass as bass
import concourse.mybir as mybir
from concourse.bass_utils import run_bass_kernel_spmd

F32 = mybir.dt.float32
BF16 = mybir.dt.bfloat16
AF = mybir.ActivationFunctionType
ALU = mybir.AluOpType

P = 128
S = 4096
D = 1024
KC = 8
T = 512
NG = S // T
NB = S // P
DFF = 2816
NFC = DFF // P
EPS = 1e-6
MEM = 256


class Atom:
    __slots__ = ("w", "r")

    def __init__(self):
        self.w = None
        self.r = []


class Buf:
    def __init__(self, atoms=None, excl=False):
        self.atoms = atoms if atoms is not None else [Atom()]
        self.excl = excl


def bufs(n):
    return [Buf() for _ in range(n)]


class DmaSem:
    def __init__(self, nc, name):
        self.sem = nc.alloc_semaphore(name)
        self.count = 0


class Op:
    __slots__ = ("e", "idx", "fn", "deps", "dma", "me", "waits", "signal")


class Sched:
    ENG = ["pe", "act", "dve", "pool", "sp"]
    NEAR = {"pe": 0, "act": 1 << 30, "dve": 1 << 30, "pool": 1 << 30, "sp": 0}

    def __init__(self, nc):
        self.nc = nc
        self.eng = {"pe": nc.tensor, "act": nc.scalar, "dve": nc.vector, "pool": nc.gpsimd, "sp": nc.sync}
        self.ops = []
        self.count = {e: 0 for e in self.ENG}
        self.sems = {e: nc.alloc_semaphore("prog_" + e) for e in self.ENG}

    def op(self, e, fn, reads=(), writes=(), dma=None):
        xr = [b for b in reads if b.excl and b not in writes]
        if xr:
            writes = list(writes) + xr
            reads = [b for b in reads if not b.excl]
        deps = set()
        for b in reads:
            for a in b.atoms:
                if a.w is not None:
                    deps.add(a.w)
        for b in writes:
            for a in b.atoms:
                if a.w is not None:
                    deps.add(a.w)
                deps.update(a.r)
        o = Op()
        o.e = e
        o.idx = self.count[e]
        self.count[e] += 1
        o.fn = fn
        o.dma = dma
        if dma is None:
            o.me = ("e", e, o.idx)
        else:
            dma.count += 16
            o.me = ("d", dma, dma.count)
        o.deps = deps
        o.waits = []
        o.signal = False
        self.ops.append(o)
        for b in reads:
            for a in b.atoms:
                a.r.append(o.me)
        for b in writes:
            for a in b.atoms:
                a.w = o.me
                a.r = []
        return o

    def finalize(self):
        known = {e: {} for e in self.ENG}
        targets = set()
        for o in self.ops:
            for d in o.deps:
                if d[0] == "e":
                    targets.add((d[1], d[2]))
        snap = {}
        opmap = {}
        for o in self.ops:
            if o.dma is None:
                opmap[(o.e, o.idx)] = o
        for o in self.ops:
            kn = known[o.e]
            need = {}
            for d in o.deps:
                if d[0] == "e":
                    _, x, j = d
                    if x == o.e:
                        if o.idx - j > self.NEAR[x]:
                            continue
                    if kn.get(x, -1) >= j:
                        continue
                    if need.get(x, -1) < j:
                        need[x] = j
                else:
                    _, ds, v = d
                    if kn.get(ds, -1) >= v:
                        continue
                    if need.get(ds, -1) < v:
                        need[ds] = v
            for k, v in need.items():
                o.waits.append((k, v))
                if kn.get(k, -1) < v:
                    kn[k] = v
                if isinstance(k, str):
                    opmap[(k, v)].signal = True
                    sn = snap.get((k, v))
                    if sn is not None:
                        for kk, vv in sn.items():
                            if kn.get(kk, -1) < vv:
                                kn[kk] = vv
            if o.dma is None:
                if (o.e, o.idx) in targets:
                    sn = dict(kn)
                    sn[o.e] = max(sn.get(o.e, -1), o.idx - 1)
                    snap[(o.e, o.idx)] = sn
        rank = {}
        cnt = {e: 0 for e in self.ENG}
        for o in self.ops:
            if o.dma is None and o.signal:
                cnt[o.e] += 1
                rank[(o.e, o.idx)] = cnt[o.e]
        nw = 0
        for o in self.ops:
            eng = self.eng[o.e]
            for k, v in o.waits:
                if isinstance(k, str):
                    eng.wait_ge(self.sems[k], rank[(k, v)])
                else:
                    eng.wait_ge(k.sem, v)
                nw += 1
            ins = o.fn()
            if o.dma is not None:
                ins.then_inc(o.dma.sem, 16)
            elif o.signal:
                ins.then_inc(self.sems[o.e], 1)
        self.stats = dict(n_ops=len(self.ops), n_waits=nw, signals=cnt)

    def final_wait(self, e, dsems):
        eng = self.eng[e]
        for ds in dsems:
            if ds.count > 0:
                eng.wait_ge(ds.sem, ds.count)


def build_program(debug_names=()):
    nc = bass.Bass("TRN2", target_bir_lowering=False)
    sc = Sched(nc)
    dbg = {}

    def dram_in(name, shape, dt=F32):
        return nc.dram_tensor(name, list(shape), dt, kind="ExternalInput").ap()

    x_d = dram_in("x", [S, D])
    mem_d = dram_in("mem", [MEM, D])
    g_mix_d = dram_in("g_mix", [D])
    w_in_d = dram_in("w_in", [D, 1792])
    b_in_d = dram_in("b_in", [1792])
    w_dw_d = dram_in("w_dw", [31, 512])
    b_dw_d = dram_in("b_dw", [512])
    g_ln_d = dram_in("g_conv_ln", [512])
    b_ln_d = dram_in("b_conv_ln", [512])
    sink_d = dram_in("attn_sink", [8])
    w_out_d = dram_in("w_out", [D, D])
    b_out_d = dram_in("b_out", [D])
    g_mq_d = dram_in("g_mem_q", [D])
    g_mkv_d = dram_in("g_mem_kv", [D])
    w_mq_d = dram_in("w_mem_q", [D, D])
    w_mkv_d = dram_in("w_mem_kv", [D, 2 * D])
    w_mo_d = dram_in("w_mem_o", [D, D])
    g_ffn_d = dram_in("g_ffn", [D])
    w_gate_d = dram_in("w_gate", [D, DFF])
    w_up_d = dram_in("w_up", [D, DFF])
    w_down_d = dram_in("w_down", [DFF, D])
    g_fin_d = dram_in("g_final", [D])
    ropec_d = dram_in("rope_cc", [S, 64])
    ropes_d = dram_in("rope_ss", [S, 64])
    ident_d = dram_in("c_ident", [P, P])
    mask_d = dram_in("c_mask", [P, 256])
    out_d = nc.dram_tensor("out", [S, D], F32, kind="ExternalOutput").ap()

    UNIT = 4096
    unit_names = (["q", "kv", "glu0", "glu1", "kvm0", "kvm1", "kvm2", "kvm3", "out0", "out1",
                   "wq0", "wq1", "wo0", "wo1"] + ["gu%d" % m for m in range(11)] + ["wd%d" % m for m in range(6)] + ["dg%d" % c for c in range(4)])
    scr = nc.dram_tensor("wscratch", [len(unit_names), P, UNIT], BF16, kind="Internal").ap()
    uidx = {n: i for i, n in enumerate(unit_names)}
    scr_buf = {n: Buf() for n in unit_names}
    scr_sem = {n: DmaSem(nc, "cv_" + n) for n in unit_names}

    def kcview(w, c0, c1):
        return w.rearrange("(kc p) n -> p kc n", p=P)[:, :, c0:c1]

    conv_hist = []

    def conv(name, dst_view, src_view):
        rd = [conv_hist[-9]] if len(conv_hist) >= 9 else []
        sc.op("pool", lambda: nc.gpsimd.dma_start(out=dst_view, in_=src_view),
              reads=rd, writes=[], dma=scr_sem[name])

    def conv_done(name):
        me = ("d", scr_sem[name], scr_sem[name].count)
        for a in scr_buf[name].atoms:
            a.w = me
        conv_hist.append(scr_buf[name])

    def unit3(name, ncols):
        return scr[uidx[name]][:, 0:KC * ncols].rearrange("p (kc n) -> p kc n", kc=KC)

    conv_order = ["kvm0", "kvm1", "kvm2", "kvm3", "out0", "out1",
                  "wq0", "wq1", "wo0", "wo1"] + ["gu%d" % m for m in range(11)] + ["wd%d" % m for m in range(6)]
    conv_emitted = [0]

    def emit_one_conv(n):
        if n in ("q", "kv"):
            c0, c1, w = (1024, 1536, 512) if n == "q" else (1536, 1792, 256)
            conv(n, unit3(n, w), kcview(w_in_d, c0, c1))
        elif n.startswith("glu"):
            u = int(n[3:])
            conv(n, unit3(n, 512)[:, :, 0:256], kcview(w_in_d, 256 * u, 256 * u + 256))
            conv(n, unit3(n, 512)[:, :, 256:512], kcview(w_in_d, 512 + 256 * u, 512 + 256 * u + 256))
        elif n.startswith("kvm"):
            u = int(n[3:])
            conv(n, unit3(n, 512), kcview(w_mkv_d, 512 * u, 512 * u + 512))
        elif n.startswith("out"):
            u = int(n[3:])
            conv(n, unit3(n, 512), kcview(w_out_d, 512 * u, 512 * u + 512))
        elif n.startswith("wq"):
            u = int(n[2:])
            conv(n, unit3(n, 512), kcview(w_mq_d, 512 * u, 512 * u + 512))
        elif n.startswith("wo"):
            u = int(n[2:])
            conv(n, unit3(n, 512), kcview(w_mo_d, 512 * u, 512 * u + 512))
        elif n.startswith("gu"):
            m = int(n[2:])
            v = scr[uidx[n]].rearrange("p (g kc n) -> p g kc n", g=2, kc=KC)
            conv(n, v[:, 0], kcview(w_gate_d, 256 * m, 256 * m + 256))
            conv(n, v[:, 1], kcview(w_up_d, 256 * m, 256 * m + 256))
        elif n.startswith("wd"):
            m = int(n[2:])
            nch = 4 if m < 5 else 2
            v = scr[uidx[n]][:, 0:nch * D].rearrange("p (c n) -> p c n", c=nch)
            src = w_down_d[m * 4 * P:(m * 4 + nch) * P, :].rearrange("(c p) n -> p c n", p=P)
            conv(n, v, src)
        conv_done(n)

    def ensure_conv(upto):
        while conv_emitted[0] <= min(upto, len(conv_order) - 1):
            emit_one_conv(conv_order[conv_emitted[0]])
            conv_emitted[0] += 1

    def sb(name, shape, dt):
        return nc.alloc_sbuf_tensor(name, list(shape), dt)

    ident = sb("ident", [P, P], BF16); ident_b = Buf()
    masks = sb("masks", [P, 2, P], BF16); masks_b = Buf()
    ones_pad = sb("ones_pad", [P, P], BF16); ones_pad_b = Buf()
    ones_full = sb("ones_full", [P, P], BF16); ones_full_b = Buf()
    bqkv_pad = sb("bqkv_pad", [P, 768], BF16); bqkv_b = Buf()
    bout_pad = sb("bout_pad", [P, D], BF16); bout_b = Buf()
    mhalf = sb("mhalf", [P, 8], F32); mhalf_b = Buf()
    gfin = sb("gfin", [P, D], F32)
    epsc = sb("epsc", [P, 8], F32)
    vec1 = sb("vec1", [P, 4, 40], F32)
    vec2 = sb("vec2", [P, KC, 4], F32)
    sinkt = sb("sinkt", [P, 8], F32)
    expsink = sb("expsink", [P, 8], F32); expsink_b = Buf()
    consts_b = Buf()
    cst0_b = Buf()
    identf = sb("identf", [P, P], F32)
    vecd_b = Buf()
    ropec = [sb("ropec%d" % i, [P, 4, 64], F32) for i in range(2)]
    ropes = [sb("ropes%d" % i, [P, 4, 64], F32) for i in range(2)]
    rope_b = bufs(2)
    rope_sem = [DmaSem(nc, "rope%d" % i) for i in range(2)]

    KmT = sb("KmT", [P, KC, MEM], BF16); KmT_b = bufs(KC)
    Vm = sb("Vm", [P, 2, D], BF16); Vm_b = [bufs(2) for _ in range(2)]

    GLW = 16 + T + 16
    gl = [sb("gl%d" % i, [P, 4, GLW], BF16) for i in range(2)]
    gl_c = [bufs(4) for _ in range(2)]
    gl_lh = bufs(2)
    gl_rh = bufs(2)
    qT = [sb("qT%d" % i, [P, 4, T], BF16) for i in range(2)]
    qT_b = [bufs(4) for _ in range(2)]
    kTz = [sb("kTz%d" % i, [P, 4, T], BF16) for i in range(3)]
    kTz_b = [bufs(4) for _ in range(3)]
    Vr = [sb("Vr%d" % i, [P, 4, 2, 65], BF16) for i in range(3)]
    Vr_b = [bufs(4) for _ in range(3)]

    xres = sb("xres", [P, 4, D], F32); xres_b = bufs(4)
    xres_sem = [DmaSem(nc, "xres%d" % i) for i in range(4)]
    out_sem = [DmaSem(nc, "outs%d" % i) for i in range(4)]
    xld = [sb("xld%d" % i, [P, D], F32) for i in range(2)]; xld_b = bufs(2)
    xld_sem = [DmaSem(nc, "xld%d" % i) for i in range(2)]
    xnb = [sb("xnb%d" % i, [P, D], BF16) for i in range(2)]; xnb_b = bufs(2)
    stat = [sb("stat%d" % i, [P, 4], F32) for i in range(2)]
    stat_b = [bufs(3) for _ in range(2)]
    hT = [sb("hT%d" % i, [P, KC, T], BF16) for i in range(3)]
    hT_b = [bufs(4) for _ in range(3)]

    NSLOT = 4
    wslot = [sb("wslot%d" % i, [P, UNIT], BF16) for i in range(NSLOT)]
    wslot_b = bufs(NSLOT)
    wslot_sem = [DmaSem(nc, "ws%d" % i) for i in range(NSLOT)]

    tgA = [sb("tgA%d" % i, [P, T], F32) for i in range(2)]; tgA_b = bufs(2)
    ahA = [sb("ahA%d" % i, [P, T], F32) for i in range(2)]; ahA_b = bufs(2)
    rtA = sb("rtA", [P, 640], F32); rtA_b = Buf()
    rtB = sb("rtB", [P, 640], F32); rtB_b = Buf()
    qrot = [sb("qrot%d" % i, [P, T], BF16) for i in range(2)]; qrot_b = bufs(2)
    kpad = [sb("kpad%d" % i, [P, 4, P], BF16) for i in range(2)]; kpad_b = bufs(2)

    ARENA_KB = 46
    arena = sb("arena", [P, ARENA_KB * 512], BF16)
    atoms = [Atom() for _ in range(ARENA_KB)]

    def carve(kb0, nkb, dt, shape_free):
        ap = arena[:, kb0 * 512:(kb0 + nkb) * 512]
        if dt == F32:
            ap = ap.bitcast(F32)
        if len(shape_free) == 2:
            ap = ap.rearrange("p (a b) -> p a b", a=shape_free[0])
        return ap

    def abuf(kb0, nkb):
        return Buf(atoms[kb0:kb0 + nkb])

    acc = carve(30, 8, F32, [4, T]); acc_b = [abuf(30 + 2 * c, 2) for c in range(4)]
    ybf = carve(38, 4, BF16, [4, T]); ybf_b = [abuf(38 + c, 1) for c in range(4)]
    ysq = carve(42, 4, BF16, [4, T]); ysq_b = [abuf(42 + c, 1) for c in range(4)]
    mean = carve(8, 2, F32, [T]); mean_b = abuf(8, 2)
    var = carve(10, 2, F32, [T]); var_b = abuf(10, 2)
    rstd = carve(12, 2, F32, [T]); rstd_b = abuf(12, 2)
    dtm = carve(14, 2, F32, [T]); dtm_b = abuf(14, 2)
    thm = carve(16, 2, F32, [T]); thm_b = abuf(16, 2)
    zhm = carve(18, 2, F32, [T]); zhm_b = abuf(18, 2)
    PT = [carve(20 + 3 * i, 3, BF16, [1536]) for i in range(2)]; PT_b = [abuf(20 + 3 * i, 3) for i in range(2)]
    yatt = carve(26, 4, BF16, [4, T]); yatt_b = [abuf(26 + t, 1) for t in range(4)]
    ymixT = carve(38, 8, BF16, [KC, T])
    ymix_b = [abuf(38 + c, 1) for c in range(KC)]
    qmT = carve(0, 8, BF16, [KC, T]); qmT_b = [abuf(c, 1) for c in range(KC)]
    PmT = [carve(8 + 2 * i, 2, BF16, [2, T]) for i in range(2)]; PmT_b = [abuf(8 + 2 * i, 2) for i in range(2)]
    rec = [carve(12 + 2 * i, 2, F32, [T]) for i in range(2)]; rec_b = [abuf(12 + 2 * i, 2) for i in range(2)]
    omT = carve(16, 8, BF16, [KC, T]); omT_b = [abuf(16 + c, 1) for c in range(KC)]
    actT = carve(0, 22, BF16, [NFC, T]); actT_b = [abuf(c, 1) for c in range(NFC)]
    thf = [carve(22 + 2 * i, 2, F32, [T]) for i in range(2)]; thf_b = [abuf(22 + 2 * i, 2) for i in range(2)]
    wvf = [carve(26 + 2 * i, 2, F32, [T]) for i in range(2)]; wvf_b = [abuf(26 + 2 * i, 2) for i in range(2)]

    NBANK = 8
    banks = [nc.alloc_psum_tensor("bank%d" % i, [P, 512], F32) for i in range(NBANK)]
    bank_b = [Buf(excl=True) for _ in range(NBANK)]
    bank_ctr = [0]

    def next_bank():
        i = bank_ctr[0] % NBANK
        bank_ctr[0] += 1
        return banks[i], bank_b[i]

    def dump(name, ap, rd):
        if name not in debug_names:
            return
        shp = list(ap.shape)
        t = nc.dram_tensor("dbg_" + name, shp, ap.dtype, kind="ExternalOutput").ap()
        ds = DmaSem(nc, "dbg_" + name)
        sc.op("sp", lambda: nc.sync.dma_start(out=t, in_=ap), reads=rd, writes=[], dma=ds)
        dbg[name] = ds

    ws_ctr = [0]

    def wload(name):
        if name in conv_order:
            ensure_conv(conv_order.index(name) + 9)
        i = ws_ctr[0] % NSLOT
        ws_ctr[0] += 1
        ln = 2048 if name in ("kv", "wd5") else (31 * P if name.startswith("dg") else UNIT)
        src = scr[uidx[name]][:, 0:ln]
        sc.op("sp", lambda: nc.sync.dma_start(out=wslot[i][:, 0:ln], in_=src),
              reads=[scr_buf[name]], writes=[wslot_b[i]], dma=wslot_sem[i])
        return wslot[i], wslot_b[i]

    csem = DmaSem(nc, "consts")
    csem2 = DmaSem(nc, "consts2")

    def emit_consts():
        sc.op("pool", lambda: nc.gpsimd.dma_start(out=ident[:], in_=ident_d), writes=[], dma=csem2)
        sc.op("pool", lambda: nc.gpsimd.dma_start(out=masks[:].rearrange("p a b -> p (a b)"), in_=mask_d), writes=[], dma=csem2)
        me = ("d", csem2, csem2.count)
        for b in (ident_b, masks_b):
            b.atoms[0].w = me
        sc.op("dve", lambda: nc.vector.memset(bqkv_pad[:], 0.0), writes=[bqkv_b])
        sc.op("dve", lambda: nc.vector.memset(bout_pad[:], 0.0), writes=[bout_b])
        sc.op("dve", lambda: nc.vector.memset(ones_pad[:], 0.0), writes=[ones_pad_b])
        sc.op("dve", lambda: nc.vector.memset(ones_full[:], 1.0), writes=[ones_full_b])
        sc.op("dve", lambda: nc.vector.memset(mhalf[:], -0.5), writes=[mhalf_b])
        sc.op("dve", lambda: nc.vector.memset(epsc[:], EPS), writes=[mhalf_b])
        sc.op("dve", lambda: nc.vector.memset(ones_pad[0:1, :], 1.0), writes=[ones_pad_b])
        bq = DmaSem(nc, "bq")
        sc.op("pool", lambda: nc.gpsimd.dma_start(out=bqkv_pad[0:1, :], in_=b_in_d[1024:1792].unsqueeze(0)),
              writes=[bqkv_b], dma=bq)
        bo = DmaSem(nc, "bo")
        sc.op("pool", lambda: nc.gpsimd.dma_start(out=bout_pad[0:1, :], in_=b_out_d.unsqueeze(0)),
              writes=[bout_b], dma=bo)
        sc.op("dve", lambda: nc.vector.memset(xld[0][:], 0.0), writes=[xld_b[0]])
        sc.op("dve", lambda: nc.vector.memset(xld[1][:], 0.0), writes=[xld_b[1]])
        st_sem = [DmaSem(nc, "stg0"), DmaSem(nc, "stg1")]
        def rload(k, dst, src):
            sc.op("sp", lambda: nc.sync.dma_start(out=dst, in_=src), reads=[xld_b[k]], writes=[], dma=st_sem[k])
        rload(0, xld[0][0:31, 0:512], w_dw_d)
        rload(0, xld[0][31:32, 0:512], b_dw_d.unsqueeze(0))
        rload(0, xld[0][32:33, 0:512], g_ln_d.unsqueeze(0))
        rload(0, xld[0][33:34, 0:512], b_ln_d.unsqueeze(0))
        rload(0, xld[0][34:35, 0:512], b_in_d[0:512].unsqueeze(0))
        rload(0, xld[0][35:36, 0:512], b_in_d[512:1024].unsqueeze(0))
        for k, g in enumerate((g_mix_d, g_mq_d, g_ffn_d, g_mkv_d)):
            rload(1, xld[1][k:k + 1, :], g.unsqueeze(0))
        sc.op("sp", lambda: nc.sync.dma_start(out=identf[:], in_=ident_d), writes=[], dma=csem)
        sc.op("sp", lambda: nc.sync.dma_start(out=gfin[:], in_=g_fin_d.partition_broadcast(P)), writes=[], dma=csem)
        sc.op("sp", lambda: nc.sync.dma_start(out=sinkt[:], in_=sink_d.partition_broadcast(P)), writes=[], dma=csem)
        me = ("d", csem, csem.count)
        cst0_b.atoms[0].w = me
        for k in range(2):
            me = ("d", st_sem[k], st_sem[k].count)
            for a in xld_b[k].atoms:
                a.w = me
        bk1, bk1b = next_bank()
        def tv1():
            ins = None
            for c in range(4):
                ins = nc.tensor.matmul(bk1[:, c * 36:(c + 1) * 36], xld[0][:, c * P:(c + 1) * P], identf[:, 0:36], start=True, stop=True)
            return ins
        sc.op("pe", tv1, reads=[xld_b[0], cst0_b], writes=[bk1b])
        bk2, bk2b = next_bank()
        def tv2():
            ins = None
            for kc in range(KC):
                ins = nc.tensor.matmul(bk2[:, kc * 4:(kc + 1) * 4], xld[1][:, kc * P:(kc + 1) * P], identf[:, 0:4], start=True, stop=True)
            return ins
        sc.op("pe", tv2, reads=[xld_b[1], cst0_b], writes=[bk2b])
        sc.op("dve", lambda: nc.vector.tensor_copy(vec1[:, :, 0:36], bk1[:, 0:144].rearrange("p (c r) -> p c r", c=4)),
              reads=[bk1b], writes=[consts_b])
        sc.op("dve", lambda: nc.vector.tensor_copy(vec2[:], bk2[:, 0:32].rearrange("p (c r) -> p c r", c=KC)),
              reads=[bk2b, cst0_b], writes=[consts_b])
        sc.op("dve", lambda: nc.vector.tensor_scalar(vec1[:, :, 36:38], vec1[:, :, 34:36], 0.5, None, ALU.mult),
              reads=[consts_b], writes=[vecd_b])
        sc.op("dve", lambda: nc.vector.tensor_scalar(vec1[:, :, 38:40], vec1[:, :, 32:34], 0.5, None, ALU.mult),
              reads=[consts_b], writes=[vecd_b])
        sc.op("act", lambda: nc.scalar.activation(expsink[:], sinkt[:], AF.Exp), reads=[cst0_b], writes=[expsink_b])

    norm_ctr = [0]
    junk = sb("junk", [P, D], BF16); junk_b = Buf()
    statg = [sb("statg%d" % i, [P, 12], F32) for i in range(3)]
    statg_b = [[bufs(4) for _ in range(3)] for _ in range(3)]
    ng_ctr = [0]

    def next_stat():
        k = ng_ctr[0] % 3
        ng_ctr[0] += 1
        return statg[k], statg_b[k]

    def norm_sq(src_ap, src_b, st, sb_, j):
        sc.op("act", lambda: nc.scalar.activation(junk[:], src_ap, AF.Square, accum_out=st[:, j:j + 1]),
              reads=[src_b], writes=[junk_b, sb_[0][j]])

    def norm_rstd(st, sb_, j0, n):
        sc.op("dve", lambda: nc.vector.tensor_scalar(st[:, 4 + j0:4 + j0 + n], st[:, j0:j0 + n], 1.0 / D, EPS, ALU.mult, ALU.add),
              reads=sb_[0][j0:j0 + n], writes=sb_[1][j0:j0 + n])
        sc.op("pool", lambda: nc.gpsimd.tensor_tensor(st[:, 8 + j0:8 + j0 + n], st[:, 4 + j0:4 + j0 + n], mhalf[:, 0:n], ALU.pow),
              reads=sb_[1][j0:j0 + n] + [mhalf_b], writes=sb_[2][j0:j0 + n])

    def norm_apply(src_ap, src_b, st, sb_, j, gcol, hT_i, tcol):
        s = norm_ctr[0] % 2
        norm_ctr[0] += 1
        sc.op("act", lambda: nc.scalar.activation(xnb[s][:], src_ap, AF.Copy, scale=st[:, 8 + j:9 + j]),
              reads=[src_b, sb_[2][j]], writes=[xnb_b[s]])
        bk, bkb = next_bank()
        pst = bk[:].bitcast(BF16).rearrange("p (k n) -> p k n", k=KC)

        def tr():
            ins = None
            for kc in range(KC):
                ins = nc.tensor.transpose(pst[:, kc, :], xnb[s][:, kc * P:(kc + 1) * P], ident[:])
            return ins
        sc.op("pe", tr, reads=[xnb_b[s], ident_b], writes=[bkb])
        dst = hT[hT_i][:, :, tcol * P:(tcol + 1) * P]
        gb = vec2[:, :, gcol:gcol + 1].to_broadcast([P, KC, P])
        sc.op("dve", lambda: nc.vector.tensor_tensor(dst, pst, gb, ALU.mult),
              reads=[bkb, consts_b], writes=[hT_b[hT_i][tcol]])

    def norm_group_pre(srcs):
        st, sb_ = next_stat()
        n = len(srcs)
        for j, (ap, b) in enumerate(srcs):
            norm_sq(ap, b, st, sb_, j)
        norm_rstd(st, sb_, 0, n)
        return (st, sb_, srcs)

    def norm_group_post(ctx, gcol, hT_i):
        st, sb_, srcs = ctx
        for j, (ap, b) in enumerate(srcs):
            norm_apply(ap, b, st, sb_, j, gcol, hT_i, j)

    def norm_group(srcs, gcol, hT_i):
        norm_group_post(norm_group_pre(srcs), gcol, hT_i)

    def mm_group(out_ap, pairs, start_first=True):
        def f():
            ins = None
            n = len(pairs)
            for k, (l, r) in enumerate(pairs):
                ins = nc.tensor.matmul(out_ap, l, r, start=(k == 0 and start_first), stop=(k == n - 1))
            return ins
        return f

    HA, HM, HF = 0, 1, 2

    def a1_xload(i, t):
        b = 4 * i + t
        xs = t % 2
        sc.op("sp", lambda: nc.sync.dma_start(out=xld[xs][:], in_=x_d[b * P:(b + 1) * P, :]),
              writes=[xld_b[xs]], dma=xld_sem[xs])

    def stage_A1_pre(i):
        rs = i % 2
        sc.op("sp", lambda: nc.sync.dma_start(out=ropec[rs][:], in_=ropec_d[i * T:(i + 1) * T, :].rearrange("(t p) c -> p t c", p=P)),
              writes=[rope_b[rs]], dma=rope_sem[rs])
        sc.op("sp", lambda: nc.sync.dma_start(out=ropes[rs][:], in_=ropes_d[i * T:(i + 1) * T, :].rearrange("(t p) c -> p t c", p=P)),
              writes=[rope_b[rs]], dma=rope_sem[rs])
        a1_xload(i, 0)
        a1_xload(i, 1)

    a1_state = {}

    def stage_A1(i, part=2):
        gslot = i % 2
        kslot = i % 3
        h_i = HA
        rs = i % 2
        if part in (0, 2):
            st, sb_ = next_stat()
            wq_ap, wq_b = wload("q")
            wkv_ap, wkv_b = wload("kv")
        else:
            st, sb_, wq_ap, wq_b, wkv_ap, wkv_b = a1_state[i]
        wq3 = wq_ap[:, 0:KC * 512].rearrange("p (kc n) -> p kc n", kc=KC)
        wkv3 = wkv_ap[:, 0:KC * 256].rearrange("p (kc n) -> p kc n", kc=KC)

        def qkv(t):
            ts_ = slice(t * P, (t + 1) * P)
            qs = t % 2
            bk, bkb = next_bank()
            pairs = [(ones_pad[:], bqkv_pad[:, 0:512])] + [(hT[h_i][:, kc, ts_], wq3[:, kc, :]) for kc in range(KC)]
            sc.op("pe", mm_group(bk[:], pairs), reads=[ones_pad_b, bqkv_b, hT_b[h_i][t], wq_b], writes=[bkb])
            bkk, bkkb = next_bank()
            pairs = [(ones_pad[:], bqkv_pad[:, 512:768])] + [(hT[h_i][:, kc, ts_], wkv3[:, kc, :]) for kc in range(KC)]
            sc.op("pe", mm_group(bkk[:, 0:256], pairs), reads=[ones_pad_b, bqkv_b, hT_b[h_i][t], wkv_b], writes=[bkkb])
            q3 = bk[:].rearrange("p (h d) -> p h d", h=8)
            cc = ropec[rs][:, t, :]
            ss = ropes[rs][:, t, :]
            a3 = rtA[:, 0:512].rearrange("p (h d) -> p h d", h=8)
            b3 = rtB[:, 0:512].rearrange("p (h d) -> p h d", h=8)
            sc.op("dve", lambda: nc.vector.tensor_tensor(a3, q3, cc.unsqueeze(1).to_broadcast([P, 8, 64]), ALU.mult),
                  reads=[bkb, rope_b[rs]], writes=[rtA_b])
            sc.op("dve", lambda: nc.vector.tensor_tensor(b3[:, :, 0:32], q3[:, :, 32:64], ss[:, 0:32].unsqueeze(1).to_broadcast([P, 8, 32]), ALU.mult),
                  reads=[bkb, rope_b[rs]], writes=[rtB_b])
            sc.op("dve", lambda: nc.vector.tensor_tensor(b3[:, :, 32:64], q3[:, :, 0:32], ss[:, 32:64].unsqueeze(1).to_broadcast([P, 8, 32]), ALU.mult),
                  reads=[bkb, rope_b[rs]], writes=[rtB_b])
            sc.op("dve", lambda: nc.vector.tensor_tensor(qrot[qs][:], rtA[:, 0:512], rtB[:, 0:512], ALU.add),
                  reads=[rtA_b, rtB_b], writes=[qrot_b[qs]])
            k3 = bkk[:, 0:128].rearrange("p (h d) -> p h d", h=2)
            v3 = bkk[:, 128:256].rearrange("p (h d) -> p h d", h=2)
            ak = rtA[:, 512:640].rearrange("p (h d) -> p h d", h=2)
            bk_ = rtB[:, 512:640].rearrange("p (h d) -> p h d", h=2)
            sc.op("dve", lambda: nc.vector.tensor_tensor(ak, k3, cc.unsqueeze(1).to_broadcast([P, 2, 64]), ALU.mult),
                  reads=[bkkb, rope_b[rs]], writes=[rtA_b])
            sc.op("dve", lambda: nc.vector.tensor_tensor(bk_[:, :, 0:32], k3[:, :, 32:64], ss[:, 0:32].unsqueeze(1).to_broadcast([P, 2, 32]), ALU.mult),
                  reads=[bkkb, rope_b[rs]], writes=[rtB_b])
            sc.op("dve", lambda: nc.vector.tensor_tensor(bk_[:, :, 32:64], k3[:, :, 0:32], ss[:, 32:64].unsqueeze(1).to_broadcast([P, 2, 32]), ALU.mult),
                  reads=[bkkb, rope_b[rs]], writes=[rtB_b])
            kp = kpad[qs][:].rearrange("p (g v) n -> p g v n", g=2)
            sc.op("dve", lambda: nc.vector.tensor_tensor(kp[:, :, 0, 0:64], ak, bk_, ALU.add),
                  reads=[rtA_b, rtB_b], writes=[kpad_b[qs]])
            sc.op("dve", lambda: nc.vector.tensor_tensor(kp[:, :, 1, 64:128], ak, bk_, ALU.add),
                  reads=[rtA_b, rtB_b], writes=[kpad_b[qs]])
            sc.op("dve", lambda: nc.vector.tensor_copy(Vr[kslot][:, t, :, 0:64], v3),
                  reads=[bkkb], writes=[Vr_b[kslot][t]])

        def rot_t(t):
            ts_ = slice(t * P, (t + 1) * P)
            qs = t % 2
            bk2, bkb2 = next_bank()
            pq = bk2[:].bitcast(BF16)[:, 0:512].rearrange("p (k n) -> p k n", k=4)

            def trq():
                ins = None
                for j in range(4):
                    ins = nc.tensor.transpose(pq[:, j, :], qrot[qs][:, j * P:(j + 1) * P], ident[:])
                return ins
            sc.op("pe", trq, reads=[qrot_b[qs], ident_b], writes=[bkb2])
            sc.op("act", lambda: nc.scalar.copy(qT[gslot][:, :, ts_], pq), reads=[bkb2], writes=[qT_b[gslot][t]])
            bk3, bkb3 = next_bank()
            pk = bk3[:].bitcast(BF16)[:, 0:512].rearrange("p (k n) -> p k n", k=4)

            def trk():
                ins = None
                for j in range(4):
                    ins = nc.tensor.transpose(pk[:, j, :], kpad[qs][:, j, :], ident[:])
                return ins
            sc.op("pe", trk, reads=[kpad_b[qs], ident_b], writes=[bkb3])
            sc.op("act", lambda: nc.scalar.copy(kTz[kslot][:, :, ts_], pk), reads=[bkb3], writes=[kTz_b[kslot][t]])

        if part in (0, 2):
            for t in range(2):
                norm_sq(xld[t][:], xld_b[t], st, sb_, t)
            norm_rstd(st, sb_, 0, 2)
            for t in range(2):
                norm_apply(xld[t][:], xld_b[t], st, sb_, t, 0, h_i, t)
            a1_xload(i, 2)
            a1_xload(i, 3)
            for t in range(2, 4):
                norm_sq(xld[t % 2][:], xld_b[t % 2], st, sb_, t)
            norm_rstd(st, sb_, 2, 2)
            a1_state[i] = (st, sb_, wq_ap, wq_b, wkv_ap, wkv_b)
        if part == 0:
            return
        qkv(0)
        qkv(1)
        for t in range(2, 4):
            norm_apply(xld[t % 2][:], xld_b[t % 2], st, sb_, t, 0, h_i, t)
        rot_t(0)
        rot_t(1)
        qkv(2)
        qkv(3)
        rot_t(2)
        rot_t(3)

    def stage_A2(i, part=2):
        gslot = i % 2
        h_i = HA
        for u in ((0, 1) if part == 2 else (part,)):
            w_ap, w_b = wload("glu%d" % u)
            w3 = w_ap[:, 0:KC * 512].rearrange("p (kc n) -> p kc n", kc=KC)
            for j in range(2):
                c = 2 * u + j
                ts2 = c % 2
                bka, bkab = next_bank()
                sc.op("pe", mm_group(bka[:], [(w3[:, kc, j * P:(j + 1) * P], hT[h_i][:, kc, :]) for kc in range(KC)]),
                      reads=[w_b] + hT_b[h_i], writes=[bkab])
                bkg, bkgb = next_bank()
                sc.op("pe", mm_group(bkg[:], [(w3[:, kc, 256 + j * P:256 + (j + 1) * P], hT[h_i][:, kc, :]) for kc in range(KC)]),
                      reads=[w_b] + hT_b[h_i], writes=[bkgb])
                sc.op("act", lambda bkg=bkg, c=c, ts2=ts2: nc.scalar.activation(tgA[ts2][:], bkg[:], AF.Tanh, bias=vec1[:, c, 37:38], scale=0.5),
                      reads=[bkgb, vecd_b], writes=[tgA_b[ts2]])
                sc.op("act", lambda bka=bka, c=c, ts2=ts2: nc.scalar.activation(ahA[ts2][:], bka[:], AF.Identity, bias=vec1[:, c, 36:37], scale=0.5),
                      reads=[bkab, vecd_b], writes=[ahA_b[ts2]])
                sc.op("dve", lambda c=c, ts2=ts2: nc.vector.scalar_tensor_tensor(gl[gslot][:, c, 16:16 + T], tgA[ts2][:], 1.0, ahA[ts2][:], ALU.add, ALU.mult),
                      reads=[tgA_b[ts2], ahA_b[ts2]], writes=[gl_c[gslot][c]])
        if part == 0:
            return
        if i == 0:
            sc.op("dve", lambda: nc.vector.memset(gl[gslot][:, :, 0:16], 0.0), writes=[gl_lh[gslot]])
        else:
            ps_ = (i - 1) % 2
            sc.op("dve", lambda: nc.vector.tensor_copy(gl[gslot][:, :, 0:16], gl[ps_][:, :, T:T + 16]),
                  reads=gl_c[ps_], writes=[gl_lh[gslot]])
            sc.op("dve", lambda: nc.vector.tensor_copy(gl[ps_][:, :, 16 + T:32 + T], gl[gslot][:, :, 16:32]),
                  reads=gl_c[gslot], writes=[gl_rh[ps_]])
        if i == NG - 1:
            sc.op("dve", lambda: nc.vector.memset(gl[gslot][:, :, 16 + T:32 + T], 0.0), writes=[gl_rh[gslot]])

    def mem_prologue():
        h_i = HM
        for t in range(2):
            sc.op("sp", lambda t=t: nc.sync.dma_start(out=xld[t][:], in_=mem_d[t * P:(t + 1) * P, :]),
                  writes=[xld_b[t]], dma=xld_sem[t])
        norm_group([(xld[t][:], xld_b[t]) for t in range(2)], 3, h_i)
        for u in range(2):
            w_ap, w_b = wload("kvm%d" % u)
            w3 = w_ap[:, 0:KC * 512].rearrange("p (kc n) -> p kc n", kc=KC)
            for j in range(4):
                oc = 4 * u + j
                bk, bkb = next_bank()
                sc.op("pe", mm_group(bk[:, 0:MEM], [(w3[:, kc, j * P:(j + 1) * P], hT[h_i][:, kc, 0:MEM]) for kc in range(KC)]),
                      reads=[w_b, hT_b[h_i][0], hT_b[h_i][1]], writes=[bkb])
                sc.op("act", lambda bk=bk, oc=oc: nc.scalar.copy(KmT[:, oc, :], bk[:, 0:MEM]),
                      reads=[bkb], writes=[KmT_b[oc]])
        for u in range(2):
            w_ap, w_b = wload("kvm%d" % (2 + u))
            w3 = w_ap[:, 0:KC * 512].rearrange("p (kc n) -> p kc n", kc=KC)
            for mt in range(2):
                bk, bkb = next_bank()
                sc.op("pe", mm_group(bk[:], [(hT[h_i][:, kc, mt * P:(mt + 1) * P], w3[:, kc, :]) for kc in range(KC)]),
                      reads=[w_b, hT_b[h_i][mt]], writes=[bkb])
                sc.op("act", lambda bk=bk, mt=mt, u=u: nc.scalar.copy(Vm[:, mt, u * 512:(u + 1) * 512], bk[:]),
                      reads=[bkb], writes=[Vm_b[mt][u]])

    def mixer(i, pre_out=None, mid=None):
        gslot = i % 2
        def conv_chunk(c):
            w_ap, w_b = wload("dg%d" % c)
            dg3 = w_ap[:, 0:31 * P].rearrange("p (j n) -> p j n", j=31)
            bk, bkb = next_bank()
            sc.op("pe", mm_group(bk[:], [(dg3[:, j, :], gl[gslot][:, c, j + 1:j + 1 + T]) for j in range(31)]),
                  reads=[w_b, gl_c[gslot][c], gl_lh[gslot], gl_rh[gslot]], writes=[bkb])
            sc.op("dve", lambda: nc.vector.tensor_scalar(acc[:, c, :], bk[:], vec1[:, c, 31:32], None, ALU.add),
                  reads=[bkb, consts_b], writes=[acc_b[c]])
        ln_steps = []

        def ln1():
            if i == 0:
                ln_cast()
            bks, bksb = next_bank()
            sc.op("pe", mm_group(bks[:], [(ones_full[:], ybf[:, c, :]) for c in range(4)]), reads=[ones_full_b] + ybf_b, writes=[bksb])
            bkq, bkqb = next_bank()
            sc.op("pe", mm_group(bkq[:], [(ones_full[:], ysq[:, c, :]) for c in range(4)]), reads=[ones_full_b] + ysq_b, writes=[bkqb])
            ln_state["bks"] = (bks, bksb)
            ln_state["bkq"] = (bkq, bkqb)

        def ln2():
            bks, bksb = ln_state["bks"]
            bkq, bkqb = ln_state["bkq"]
            sc.op("dve", lambda: nc.vector.tensor_scalar(mean, bks[:], 1.0 / 512, None, ALU.mult), reads=[bksb], writes=[mean_b])
            sc.op("dve", lambda: nc.vector.tensor_tensor(var, mean, mean, ALU.mult), reads=[mean_b], writes=[var_b])
            sc.op("dve", lambda: nc.vector.scalar_tensor_tensor(var, bkq[:], 1.0 / 512, var, ALU.mult, ALU.subtract), reads=[bkqb, var_b], writes=[var_b])
            sc.op("act", lambda: nc.scalar.activation(var, var, AF.Sqrt, bias=epsc[:, 0:1]), reads=[var_b, mhalf_b], writes=[var_b])

        def ln3():
            sc.op("dve", lambda: nc.vector.reciprocal(rstd, var), reads=[var_b], writes=[rstd_b])

        def ln_chunk(c):
            def f():
                sc.op("dve", lambda: nc.vector.tensor_tensor(dtm, acc[:, c, :], mean, ALU.subtract), reads=[acc_b[c], mean_b], writes=[dtm_b])
                sc.op("dve", lambda: nc.vector.tensor_tensor(dtm, dtm, rstd, ALU.mult), reads=[dtm_b, rstd_b], writes=[dtm_b])
                sc.op("act", lambda: nc.scalar.activation(thm, dtm, AF.Tanh, bias=vec1[:, c, 39:40], scale=vec1[:, c, 38:39]),
                      reads=[dtm_b, vecd_b], writes=[thm_b])
                sc.op("dve", lambda: nc.vector.tensor_scalar(zhm, dtm, vec1[:, c, 38:39], vec1[:, c, 39:40], ALU.mult, ALU.add),
                      reads=[dtm_b, vecd_b], writes=[zhm_b])
                sc.op("dve", lambda: nc.vector.scalar_tensor_tensor(ymixT[:, c, :], thm, 1.0, zhm, ALU.add, ALU.mult),
                      reads=[thm_b, zhm_b], writes=[ymix_b[c]])
            return f
        ln_state = {}
        ln_steps = [ln1, ln2, ln3] + [ln_chunk(c) for c in range(4)]
        kbs = [kb for kb in range(4 * i - 1, 4 * i + 5) if 0 <= kb < NB]
        tiles = []
        for kb in kbs:
            qlo = max(kb - 1, 4 * i)
            qhi = min(kb + 1, 4 * i + 3)
            tiles.append((kb, qlo, qhi, (qhi - qlo + 1) * P))
        binfill = []
        place = {}
        for (kb, qlo, qhi, n) in sorted(tiles, key=lambda x: -x[3]):
            for bi in range(len(binfill)):
                if binfill[bi] + n <= 512:
                    place[kb] = (bi, binfill[bi])
                    binfill[bi] += n
                    break
            else:
                place[kb] = (len(binfill), 0)
                binfill.append(n)
        col0 = {}
        for (kb, qlo, qhi, n) in tiles:
            bi, off = place[kb]
            for qb in range(qlo, qhi + 1):
                col0[(kb, qb)] = bi * 512 + off + (qb - qlo) * P

        def scores(h):
            g = h // 4
            pr = h // 2
            va = h % 2
            pb_i = h % 2
            PTh = PT[pb_i]
            sbanks = [next_bank() for _ in binfill]
            for (kb, qlo, qhi, n) in tiles:
                bi, off = place[kb]
                bk, bkb = sbanks[bi]
                ksl = (kb // 4) % 3
                kt = kb % 4
                lhsT = kTz[ksl][:, g * 2 + va, kt * P:(kt + 1) * P]
                rhs = qT[gslot][:, pr, (qlo - 4 * i) * P:(qhi - 4 * i + 1) * P]
                side = []
                for qb in range(qlo, qhi + 1):
                    if qb == kb + 1:
                        side.append((off + (qb - qlo) * P, 0))
                    elif qb == kb - 1:
                        side.append((off + (qb - qlo) * P, 1))

                def smm(bk=bk, off=off, n=n, lhsT=lhsT, rhs=rhs, side=side):
                    ins = nc.tensor.matmul(bk[:, off:off + n], lhsT, rhs, start=True, stop=(len(side) == 0))
                    for k_, (co, wm) in enumerate(side):
                        ins = nc.tensor.matmul(bk[:, co:co + P], ident[:], masks[:, wm, :], start=False, stop=(k_ == len(side) - 1))
                    return ins
                sc.op("pe", smm,
                      reads=[kTz_b[ksl][kt], ident_b, masks_b] + [qT_b[gslot][q - 4 * i] for q in range(qlo, qhi + 1)], writes=[bkb])
            for bi, fill in enumerate(binfill):
                bk, bkb = sbanks[bi]
                sc.op("act", lambda bk=bk, bi=bi, fill=fill, PTh=PTh: nc.scalar.activation(PTh[:, bi * 512:bi * 512 + fill], bk[:, 0:fill], AF.Exp, scale=0.125),
                      reads=[bkb], writes=[PT_b[pb_i]])

        def pvout(h):
            g = h // 4
            pb_i = h % 2
            PTh = PT[pb_i]
            bko, bkob = next_bank()
            po = bko[:, 0:260].rearrange("p (q d) -> p q d", q=4)

            def pv():
                ins = None
                for ql in range(4):
                    qb = 4 * i + ql
                    kl = [kb for kb in (qb - 1, qb, qb + 1) if 0 <= kb < NB]
                    for n_, kb in enumerate(kl):
                        cq = col0[(kb, qb)]
                        ins = nc.tensor.matmul(po[:, ql, :], PTh[:, cq:cq + P], Vr[(kb // 4) % 3][:, kb % 4, g, :],
                                               start=(n_ == 0), stop=(n_ == len(kl) - 1))
                return ins
            vrd = [Vr_b[(kb // 4) % 3][kb % 4] for kb in kbs]
            sc.op("pe", pv, reads=[PT_b[pb_i]] + vrd, writes=[bkob])
            dn = dens[h % 2]
            dnb = dens_b[h % 2]
            sc.op("dve", lambda: nc.vector.tensor_scalar(dn[:, 0:4], po[:, :, 64], expsink[:, h:h + 1], None, ALU.add),
                  reads=[bkob, expsink_b], writes=[dnb])
            sc.op("dve", lambda: nc.vector.reciprocal(dn[:, 4:8], dn[:, 0:4]), reads=[dnb], writes=[dnb])
            ya = yatt[:, :, h * 64:(h + 1) * 64]
            sc.op("dve", lambda: nc.vector.tensor_tensor(ya, po[:, :, 0:64], dn[:, 4:8].unsqueeze(2).to_broadcast([P, 4, 64]), ALU.mult),
                  reads=[bkob, dnb], writes=yatt_b)

        if i == 0:
            for c in range(4):
                conv_chunk(c)
        scores(0)
        scores(1)
        yield
        for h in range(8):
            if 1 <= h and h + 1 < 8:
                scores(h + 1)
            if h < len(ln_steps):
                ln_steps[h]()
            pvout(h)
            if h == 2 and mid is not None:
                mid()
        for st_ in ln_steps[8:]:
            st_()
        if pre_out is not None:
            pre_out()
        for ql in range(4):
            bk, bkb = next_bank()
            pt_ = bk[:].bitcast(BF16)[:, 0:512].rearrange("p (k n) -> p k n", k=4)

            def tra(pt_=pt_, ql=ql):
                ins = None
                for j in range(4):
                    ins = nc.tensor.transpose(pt_[:, j, :], yatt[:, ql, j * P:(j + 1) * P], ident[:])
                return ins
            sc.op("pe", tra, reads=[yatt_b[ql], ident_b], writes=[bkb])
            sc.op("dve", lambda pt_=pt_, ql=ql: nc.vector.tensor_copy(ymixT[:, 4:8, ql * P:(ql + 1) * P], pt_),
                  reads=[bkb], writes=ymix_b[4:8])
        wl = [wload("out%d" % u) for u in range(2)]
        for t in range(4):
            b = 4 * i + t
            sc.op("sp", lambda t=t, b=b: nc.sync.dma_start(out=xres[:, t, :], in_=x_d[b * P:(b + 1) * P, :]),
                  writes=[xres_b[t]], dma=xres_sem[t])
        for t in range(4):
            for u in range(2):
                w_ap, w_b = wl[u]
                w3 = w_ap[:, 0:KC * 512].rearrange("p (kc n) -> p kc n", kc=KC)
                bk, bkb = next_bank()
                pairs = [(ones_pad[:], bout_pad[:, u * 512:(u + 1) * 512])] + [(ymixT[:, kc, t * P:(t + 1) * P], w3[:, kc, :]) for kc in range(KC)]
                sc.op("pe", mm_group(bk[:], pairs), reads=[ones_pad_b, bout_b, w_b] + ymix_b, writes=[bkb])
                xs_ = xres[:, t, u * 512:(u + 1) * 512]
                sc.op("dve", lambda bk=bk, xs_=xs_: nc.vector.tensor_tensor(xs_, bk[:], xs_, ALU.add),
                      reads=[bkb, xres_b[t]], writes=[xres_b[t]])
        dump("x1_%d" % i, xres[:], xres_b)

    def memattn_pre(i):
        return norm_group_pre([(xres[:, t, :], xres_b[t]) for t in range(4)])

    def memattn(i, ctx, filler=None):
        h_i = HM
        norm_group_post(ctx, 1, h_i)
        for u in range(2):
            w_ap, w_b = wload("wq%d" % u)
            w3 = w_ap[:, 0:KC * 512].rearrange("p (kc n) -> p kc n", kc=KC)
            for j in range(4):
                oc = 4 * u + j
                bk, bkb = next_bank()
                sc.op("pe", mm_group(bk[:], [(w3[:, kc, j * P:(j + 1) * P], hT[h_i][:, kc, :]) for kc in range(KC)]),
                      reads=[w_b] + hT_b[h_i], writes=[bkb])
                sc.op("act", lambda bk=bk, oc=oc: nc.scalar.copy(qmT[:, oc, :], bk[:]), reads=[bkb], writes=[qmT_b[oc]])
        def m_scores(hm):
            pi = hm % 2
            for mt in range(2):
                bk, bkb = next_bank()
                sc.op("pe", mm_group(bk[:], [(KmT[:, 2 * hm + dc, mt * P:(mt + 1) * P], qmT[:, 2 * hm + dc, :]) for dc in range(2)]),
                      reads=[KmT_b[2 * hm], KmT_b[2 * hm + 1], qmT_b[2 * hm], qmT_b[2 * hm + 1]], writes=[bkb])
                sc.op("act", lambda bk=bk, pi=pi, mt=mt: nc.scalar.activation(PmT[pi][:, mt, :], bk[:], AF.Exp, scale=1.0 / 16),
                      reads=[bkb], writes=[PmT_b[pi]])

        def m_pv(hm):
            pi = hm % 2
            bk, bkb = next_bank()
            sc.op("pe", mm_group(bk[:], [(ones_full[:], PmT[pi][:, mt, :]) for mt in range(2)]), reads=[ones_full_b, PmT_b[pi]], writes=[bkb])
            sc.op("dve", lambda bk=bk, pi=pi: nc.vector.reciprocal(rec[pi], bk[:]), reads=[bkb], writes=[rec_b[pi]])
            for dc in range(2):
                oc = 2 * hm + dc
                bk, bkb = next_bank()
                sc.op("pe", mm_group(bk[:], [(Vm[:, mt, oc * P:(oc + 1) * P], PmT[pi][:, mt, :]) for mt in range(2)]),
                      reads=[Vm_b[0][oc // 4], Vm_b[1][oc // 4], PmT_b[pi]], writes=[bkb])
                sc.op("dve", lambda bk=bk, oc=oc, pi=pi: nc.vector.tensor_tensor(omT[:, oc, :], bk[:], rec[pi], ALU.mult),
                      reads=[bkb, rec_b[pi]], writes=[omT_b[oc]])

        m_scores(0)
        for hm in range(4):
            if hm + 1 < 4:
                m_scores(hm + 1)
            m_pv(hm)
        if filler is not None:
            filler()
        wl = [wload("wo%d" % u) for u in range(2)]
        for t in range(4):
            for u in range(2):
                w_ap, w_b = wl[u]
                w3 = w_ap[:, 0:KC * 512].rearrange("p (kc n) -> p kc n", kc=KC)
                bk, bkb = next_bank()
                sc.op("pe", mm_group(bk[:], [(omT[:, kc, t * P:(t + 1) * P], w3[:, kc, :]) for kc in range(KC)]),
                      reads=[w_b] + omT_b, writes=[bkb])
                xs_ = xres[:, t, u * 512:(u + 1) * 512]
                sc.op("dve", lambda bk=bk, xs_=xs_: nc.vector.tensor_tensor(xs_, bk[:], xs_, ALU.add),
                      reads=[bkb, xres_b[t]], writes=[xres_b[t]])
        dump("x2_%d" % i, xres[:], xres_b)

    def ffn_pre(i):
        return norm_group_pre([(xres[:, t, :], xres_b[t]) for t in range(4)])

    def ln_cast():
        for c in range(4):
            sc.op("act", lambda c=c: nc.scalar.copy(ybf[:, c, :], acc[:, c, :]), reads=[acc_b[c]], writes=[ybf_b[c]])
            sc.op("pool", lambda c=c: nc.gpsimd.tensor_tensor(ysq[:, c, :], acc[:, c, :], acc[:, c, :], ALU.mult), reads=[acc_b[c]], writes=[ysq_b[c]])

    def conv_ops(g):
        gs = g % 2
        ops = []
        for j in range(31):
            for c in range(4):
                if j == 0:
                    def f(c=c):
                        sc.op("dve", lambda: nc.vector.tensor_scalar(acc[:, c, :], gl[gs][:, c, 1:1 + T], vec1[:, c, 0:1], vec1[:, c, 31:32], ALU.mult, ALU.add),
                              reads=[gl_c[gs][c], gl_lh[gs], consts_b], writes=[acc_b[c]])
                else:
                    def f(c=c, j=j):
                        rd = [gl_c[gs][c], acc_b[c]]
                        if j < 15:
                            rd.append(gl_lh[gs])
                        if j > 15:
                            rd.append(gl_rh[gs])
                        sc.op("dve", lambda: nc.vector.scalar_tensor_tensor(acc[:, c, :], gl[gs][:, c, j + 1:j + 1 + T], vec1[:, c, j:j + 1], acc[:, c, :], ALU.mult, ALU.add),
                              reads=rd, writes=[acc_b[c]])
                ops.append(f)
        return ops

    def ffn(i, ctx):
        h_i = HF
        norm_group_post(ctx, 2, h_i)
        cops = conv_ops(i + 1) if i + 1 < NG else []
        cpos = [0]

        def emit_conv(n):
            for _ in range(n):
                if cpos[0] < len(cops):
                    cops[cpos[0]]()
                    cpos[0] += 1
        for m in range(11):
            w_ap, w_b = wload("gu%d" % m)
            w4 = w_ap[:].rearrange("p (g kc n) -> p g kc n", g=2, kc=KC)
            for j in range(2):
                c = 2 * m + j
                fs = c % 2
                bkg, bkgb = next_bank()
                sc.op("pe", mm_group(bkg[:], [(w4[:, 0, kc, j * P:(j + 1) * P], hT[h_i][:, kc, :]) for kc in range(KC)]),
                      reads=[w_b] + hT_b[h_i], writes=[bkgb])
                bku, bkub = next_bank()
                sc.op("pe", mm_group(bku[:], [(w4[:, 1, kc, j * P:(j + 1) * P], hT[h_i][:, kc, :]) for kc in range(KC)]),
                      reads=[w_b] + hT_b[h_i], writes=[bkub])
                sc.op("act", lambda bkg=bkg, fs=fs: nc.scalar.activation(thf[fs], bkg[:], AF.Silu),
                      reads=[bkgb], writes=[thf_b[fs]])
                sc.op("dve", lambda bku=bku, fs=fs, c=c: nc.vector.tensor_tensor(actT[:, c, :], bku[:], thf[fs], ALU.mult),
                      reads=[thf_b[fs], bkub], writes=[actT_b[c]])
                emit_conv(4)
        emit_conv(len(cops))
        if i + 1 < NG:
            ln_cast()
        accb = {}
        for t in range(4):
            for u in range(2):
                accb[(t, u)] = next_bank()
        for m in range(6):
            w_ap, w_b = wload("wd%d" % m)
            nch = 4 if m < 5 else 2
            w3 = w_ap[:, 0:nch * D].rearrange("p (c n) -> p c n", c=nch)
            for cl in range(nch):
                c = 4 * m + cl
                for t in range(4):
                    for u in range(2):
                        bk, bkb = accb[(t, u)]

                        def f(bk=bk, c=c, t=t, u=u, w3=w3, cl=cl):
                            return nc.tensor.matmul(bk[:], actT[:, c, t * P:(t + 1) * P], w3[:, cl, u * 512:(u + 1) * 512],
                                                    start=(c == 0), stop=(c == NFC - 1))
                        sc.op("pe", f, reads=[actT_b[c], w_b], writes=[bkb])
        for t in range(4):
            for u in range(2):
                bk, bkb = accb[(t, u)]
                xs_ = xres[:, t, u * 512:(u + 1) * 512]
                sc.op("dve", lambda bk=bk, xs_=xs_: nc.vector.tensor_tensor(xs_, bk[:], xs_, ALU.add),
                      reads=[bkb, xres_b[t]], writes=[xres_b[t]])
        dump("x3_%d" % i, xres[:], xres_b)

    fin_state = {}

    def final_pre(i):
        st, sb_ = next_stat()
        for t in range(4):
            norm_sq(xres[:, t, :], xres_b[t], st, sb_, t)
        norm_rstd(st, sb_, 0, 4)
        fin_state[i] = (st, sb_)

    def final_post(i):
        st, sb_ = fin_state[i]
        for t in range(4):
            b = 4 * i + t
            sc.op("dve", lambda t=t: nc.vector.scalar_tensor_tensor(xres[:, t, :], xres[:, t, :], st[:, 8 + t:9 + t], gfin[:], ALU.mult, ALU.mult),
                  reads=[xres_b[t], sb_[2][t], consts_b], writes=[xres_b[t]])
            sc.op("act", lambda t=t, b=b: nc.scalar.dma_start(out=out_d[b * P:(b + 1) * P, :], in_=xres[:, t, :]),
                  reads=[xres_b[t]], writes=[], dma=out_sem[t])
            for a_ in xres_b[t].atoms:
                a_.r.append(("d", out_sem[t], out_sem[t].count))

    dens = [sb("dens%d" % i, [P, 8], F32) for i in range(2)]
    dens_b = bufs(2)

    fast_state = {}

    def emit_fast_units(first):
        stg = [arena[:, k * 8192:(k + 1) * 8192].bitcast(F32) for k in range(2)]
        stg_b = [Buf(atoms[16 * k:16 * k + 16]) for k in range(2)]
        ubf = arena[:, 16384:16384 + UNIT]
        ubf_b = Buf(atoms[32:40])
        if first:
            fast_state["fsem"] = [DmaSem(nc, "fast_ld%d" % k) for k in range(2)]
            fast_state["ssem"] = DmaSem(nc, "fast_st")
            fast_state["stg_b"] = stg_b
            fast_state["ubf_b"] = ubf_b
        fsem = fast_state["fsem"]
        ssem = fast_state["ssem"]
        stg_b = fast_state["stg_b"]
        ubf_b = fast_state["ubf_b"]
        plan = [("q", [(1024, 1536, 0)], 512), ("kv", [(1536, 1792, 0)], 256),
                ("glu0", [(0, 256, 0), (512, 768, 256)], 512), ("glu1", [(256, 512, 0), (768, 1024, 256)], 512)]
        for k, (n, parts, w) in enumerate(plan):
            if (k < 2) != first:
                continue
            sl = k % 2
            s3 = stg[sl][:, 0:KC * w].rearrange("p (kc n) -> p kc n", kc=KC)
            for (c0, c1, d0) in parts:
                sc.op("sp", lambda s3=s3, c0=c0, c1=c1, d0=d0: nc.sync.dma_start(out=s3[:, :, d0:d0 + (c1 - c0)], in_=kcview(w_in_d, c0, c1)),
                      writes=[stg_b[sl]], dma=fsem[sl])
            sc.op("dve", lambda sl=sl, w=w: nc.vector.tensor_copy(ubf[:, 0:KC * w], stg[sl][:, 0:KC * w]),
                  reads=[stg_b[sl]], writes=[ubf_b])
            sc.op("sp", lambda n=n, w=w: nc.sync.dma_start(out=scr[uidx[n]][:, 0:KC * w], in_=ubf[:, 0:KC * w]),
                  reads=[ubf_b], writes=[], dma=ssem)
            me = ("d", ssem, ssem.count)
            for a in scr_buf[n].atoms:
                a.w = me
            for a in ubf_b.atoms:
                a.r.append(me)

    diag_state = {}

    def emit_diag_build():
        stg = arena[:, 0:4 * 31 * P].rearrange("p (c j n) -> p c j n", c=4, j=31)
        stg_b = Buf(atoms[0:31])
        for c in range(4):
            sc.op("dve", lambda c=c: nc.vector.tensor_tensor(
                stg[:, c], ident[:].unsqueeze(1).to_broadcast([P, 31, P]),
                vec1[:, c, 0:31].unsqueeze(2).to_broadcast([P, 31, P]), ALU.mult),
                reads=[ident_b, consts_b], writes=[stg_b])
        diag_state["stg"] = (stg, stg_b)

    def emit_diag_store():
        stg, stg_b = diag_state["stg"]
        for c in range(4):
            n = "dg%d" % c
            sc.op("sp", lambda c=c, n=n: nc.sync.dma_start(out=scr[uidx[n]][:, 0:31 * P], in_=stg[:, c].rearrange("p j n -> p (j n)")),
                  reads=[stg_b], writes=[], dma=scr_sem[n])
            me = ("d", scr_sem[n], scr_sem[n].count)
            for a in scr_buf[n].atoms:
                a.w = me
            for a in stg_b.atoms:
                a.r.append(me)

    emit_consts()
    stage_A1_pre(0)
    emit_fast_units(True)
    for k in range(3):
        sc.op("dve", lambda k=k: nc.vector.memset(Vr[k][:], 1.0), writes=Vr_b[k])
    for k in range(2):
        sc.op("dve", lambda k=k: nc.vector.memset(kpad[k][:], 0.0), writes=[kpad_b[k]])
    stage_A1(0)
    emit_fast_units(False)
    ensure_conv(5)
    emit_diag_build()
    if NG > 1:
        stage_A1_pre(1)
    stage_A2(0)
    emit_diag_store()
    if NG > 1:
        stage_A1(1)
        stage_A2(1)
    def mk_mixer(i):
        return mixer(i, pre_out=(lambda: stage_A1(i + 2, 0)) if i + 2 < NG else None,
                     mid=(lambda: final_post(i - 1)) if i > 0 else None)

    g = mk_mixer(0)
    next(g)
    for i in range(NG):
        if i + 2 < NG:
            stage_A1_pre(i + 2)
        for _ in g:
            pass
        ctx = memattn_pre(i)
        if i + 2 < NG:
            stage_A1(i + 2, 1)
        if i == 0:
            mem_prologue()
        memattn(i, ctx, filler=(lambda i=i: stage_A2(i + 2, 0)) if i + 2 < NG else None)
        ctx = ffn_pre(i)
        if i + 2 < NG:
            stage_A2(i + 2, 1)
        ffn(i, ctx)
        if i + 1 < NG:
            g = mk_mixer(i + 1)
            next(g)
            final_pre(i)
        else:
            final_pre(i)
            final_post(i)
    sc.finalize()
    sc.final_wait("act", out_sem)
    sc.final_wait("sp", list(dbg.values()))
    return nc, sc


def _consts():
    pos = np.arange(S, dtype=np.float64)
    inv_freq = 10000.0 ** (-np.arange(0, 64, 2, dtype=np.float64) / 64.0)
    ang = pos[:, None] * inv_freq[None, :]
    c = np.cos(ang).astype(np.float32)
    s = np.sin(ang).astype(np.float32)
    cc = np.concatenate([c, c], axis=1)
    ss = np.concatenate([-s, s], axis=1)
    ident = np.eye(P, dtype=np.float32)
    cidx = np.arange(P)[:, None]
    aidx = np.arange(P)[None, :]
    m0 = np.where(cidx >= aidx, 0.0, -30000.0).astype(np.float32)
    m1 = np.where(cidx <= aidx, 0.0, -30000.0).astype(np.float32)
    mask = np.concatenate([m0, m1], axis=1)
    return dict(rope_cc=np.ascontiguousarray(cc), rope_ss=np.ascontiguousarray(ss), c_ident=ident,
                c_mask=np.ascontiguousarray(mask))


_CACHE = {}


def kernel(**inputs):
    if "prog" not in _CACHE:
        _CACHE["prog"] = build_program()
    nc, sc = _CACHE["prog"]
    f = lambda k: np.ascontiguousarray(np.asarray(inputs[k], dtype=np.float32))
    shared = dict(
        g_mix=f("g_mix").reshape(D), w_in=f("w_in").reshape(D, 1792), b_in=f("b_in").reshape(1792),
        w_dw=f("w_dw").reshape(31, 512), b_dw=f("b_dw").reshape(512), g_conv_ln=f("g_conv_ln").reshape(512),
        b_conv_ln=f("b_conv_ln").reshape(512), attn_sink=f("attn_sink").reshape(8),
        w_out=f("w_out").reshape(D, D), b_out=f("b_out").reshape(D), g_mem_q=f("g_mem_q").reshape(D),
        g_mem_kv=f("g_mem_kv").reshape(D), w_mem_q=f("w_mem_q").reshape(D, D),
        w_mem_kv=f("w_mem_kv").reshape(D, 2 * D), w_mem_o=f("w_mem_o").reshape(D, D),
        g_ffn=f("g_ffn").reshape(D), w_gate=f("w_gate").reshape(D, DFF), w_up=f("w_up").reshape(D, DFF),
        w_down=f("w_down").reshape(DFF, D), g_final=f("g_final").reshape(D),
    )
    shared.update(_consts())
    x = f("x")
    mem = f("mem")
    in_maps = []
    for c in range(8):
        m = dict(shared)
        m["x"] = np.ascontiguousarray(x[c])
        m["mem"] = np.ascontiguousarray(mem[c])
        in_maps.append(m)
    res = run_bass_kernel_spmd(nc, in_maps, core_ids=list(range(8)))
    out = np.stack([np.asarray(res.results[c]["out"], dtype=np.float32) for c in range(8)], axis=0)
    return out
```
